# Optimizing a Trainium2 kernel written in Bass

```python
import math
import jax, jax.numpy as jnp
from jax import lax
import numpy as np

D_MODEL = 1024
BATCH = 2
SEQ = 8192
DEPTH = 1

ATTN_HEADS = 8
HEAD_DIM = 64
ATTN_W = ATTN_HEADS * HEAD_DIM
MOBA_BLOCK = 256
MOBA_TOPK = 3
Q_CHUNK = 64
POOL_GROUPS = 4
POOL_GROUP_W = 128
POOL_W = POOL_GROUPS * POOL_GROUP_W
POOL_WINDOWS = (2, 4, 8, 16)
NUM_BUCKETS = 32
MAX_DISTANCE = 128
D_FF = 2816
CONV_W = 3
EPS = 1e-6
IN_W = 3 * ATTN_W + POOL_W + 2 * D_MODEL

kernel_name = "hybrid_moba_pool_convffn_block"


def rms_norm(x, gain):
    xf = x.astype(jnp.float32)
    y = xf * lax.rsqrt(jnp.mean(xf * xf, axis=-1, keepdims=True) + EPS)
    return (y * gain.astype(jnp.float32)).astype(x.dtype)


def t5_bucket(dist):
    max_exact = NUM_BUCKETS // 2
    n = jnp.maximum(dist, 0)
    nf = jnp.maximum(n, 1).astype(jnp.float32)
    large = max_exact + (jnp.log(nf / max_exact) / math.log(MAX_DISTANCE / max_exact)
                         * (NUM_BUCKETS - max_exact)).astype(jnp.int32)
    large = jnp.minimum(large, NUM_BUCKETS - 1)
    return jnp.where(n < max_exact, n, large)


def moba_attention(q, k, v, rel_bias):
    B, H, S, Dh = q.shape
    n_blocks = -(-S // MOBA_BLOCK)
    s_pad = n_blocks * MOBA_BLOCK
    pad = ((0, 0), (0, 0), (0, s_pad - S), (0, 0))
    q = jnp.pad(q, pad)
    k = jnp.pad(k, pad)
    v = jnp.pad(v, pad)
    k_blocks = k.reshape(B, H, n_blocks, MOBA_BLOCK, Dh)
    v_blocks = v.reshape(B, H, n_blocks, MOBA_BLOCK, Dh)
    k_mean = jnp.mean(k_blocks.astype(jnp.float32), axis=3).astype(k.dtype)

    pos = jnp.arange(s_pad)
    q_blk = pos // MOBA_BLOCK
    gate = jnp.einsum('bhsd,bhnd->bhsn', q, k_mean).astype(jnp.float32)
    past = jnp.arange(n_blocks)[None, :] < q_blk[:, None]
    gate = jnp.where(past[None, None], gate, -jnp.inf)
    k_sel = min(MOBA_TOPK, n_blocks)
    _, sel = lax.top_k(gate, k_sel)
    sel_valid = jnp.arange(k_sel)[None, :] < q_blk[:, None]

    n_chunks = s_pad // Q_CHUNK
    q_c = q.reshape(B, H, n_chunks, Q_CHUNK, Dh).transpose(2, 0, 1, 3, 4)
    sel_c = sel.reshape(B, H, n_chunks, Q_CHUNK, k_sel).transpose(2, 0, 1, 3, 4)
    valid_c = sel_valid.reshape(n_chunks, Q_CHUNK, k_sel)
    bias_t = rel_bias.T.astype(jnp.float32)
    b_idx = jnp.arange(B)[:, None, None, None]
    h_idx = jnp.arange(H)[None, :, None, None]
    offs = jnp.arange(MOBA_BLOCK)
    scale = HEAD_DIM ** -0.5
    n_sel = k_sel * MOBA_BLOCK

    def chunk_fn(args):
        ci, qc, selc, validc = args
        qpos = ci * Q_CHUNK + jnp.arange(Q_CHUNK)
        own = (ci * Q_CHUNK) // MOBA_BLOCK
        k_own = lax.dynamic_index_in_dim(k_blocks, own, axis=2, keepdims=False)
        v_own = lax.dynamic_index_in_dim(v_blocks, own, axis=2, keepdims=False)
        dist_own = qpos[:, None] - (own * MOBA_BLOCK + offs)[None, :]
        bias_own = bias_t[:, t5_bucket(dist_own)]
        logit_own = jnp.einsum('bhqd,bhkd->bhqk', qc, k_own).astype(jnp.float32) * scale + bias_own
        logit_own = jnp.where(dist_own >= 0, logit_own, -jnp.inf)
        k_g = k_blocks[b_idx, h_idx, selc]
        v_g = v_blocks[b_idx, h_idx, selc]
        kpos_sel = selc[..., None] * MOBA_BLOCK + offs
        dist_sel = qpos[None, None, :, None, None] - kpos_sel
        bias_sel = bias_t[h_idx[..., None], t5_bucket(dist_sel)]
        logit_sel = jnp.einsum('bhqd,bhqnkd->bhqnk', qc, k_g).astype(jnp.float32) * scale + bias_sel
        logit_sel = jnp.where(validc[None, None, :, :, None], logit_sel, -jnp.inf)
        logits = jnp.concatenate([logit_sel.reshape(B, H, Q_CHUNK, n_sel), logit_own], axis=-1)
        prob = jax.nn.softmax(logits, axis=-1)
        p_sel = prob[..., :n_sel].reshape(B, H, Q_CHUNK, k_sel, MOBA_BLOCK).astype(v.dtype)
        p_own = prob[..., n_sel:].astype(v.dtype)
        return (jnp.einsum('bhqnk,bhqnkd->bhqd', p_sel, v_g)
                + jnp.einsum('bhqk,bhkd->bhqd', p_own, v_own))

    out = lax.map(chunk_fn, (jnp.arange(n_chunks), q_c, sel_c, valid_c))
    out = out.transpose(1, 2, 0, 3, 4).reshape(B, H, s_pad, Dh)
    return out[:, :, :S]


def multiscale_pool(p):
    B, S, G, Cg = p.shape
    pf = p.astype(jnp.float32)
    cs = jnp.concatenate([jnp.zeros((B, 1, G, Cg), jnp.float32), jnp.cumsum(pf, axis=1)], axis=1)
    win = jnp.array(POOL_WINDOWS, jnp.int32)
    t = jnp.arange(S)
    lo = jnp.maximum(t[:, None] + 1 - win[None, :], 0)
    cs_lo = cs[:, lo, jnp.arange(G)[None, :], :]
    count = jnp.minimum(t[:, None] + 1, win[None, :]).astype(jnp.float32)
    mean = (cs[:, 1:] - cs_lo) / count[None, :, :, None]
    return (mean - pf).astype(p.dtype)


def causal_dwconv(u, w, b):
    C = u.shape[-1]
    y = lax.conv_general_dilated(u, w[:, None, :].astype(u.dtype), window_strides=(1,),
                                 padding=((CONV_W - 1, 0),),
                                 dimension_numbers=('NWC', 'WIO', 'NWC'),
                                 feature_group_count=C)
    return y + b


def setup_inputs(seed: int = 0) -> dict:
    key = jax.random.key(seed)
    ks = jax.random.split(key, 20)
    L, D = DEPTH, D_MODEL
    nrm = lambda k, shape, s: jax.random.normal(k, shape, jnp.float32) * s
    return {
        "x": nrm(ks[0], (BATCH, SEQ, D), 1.0),
        "c": nrm(ks[1], (BATCH, D), 1.0),
        "ada_w": nrm(ks[2], (L, D, 6 * D), 0.1 * D ** -0.5),
        "ada_b": nrm(ks[3], (L, 6 * D), 0.02),
        "norm1_g": 1.0 + nrm(ks[4], (L, D), 0.05),
        "w_in": nrm(ks[5], (L, D, IN_W), D ** -0.5),
        "q_norm_g": 1.0 + nrm(ks[6], (L, HEAD_DIM), 0.05),
        "k_norm_g": 1.0 + nrm(ks[7], (L, HEAD_DIM), 0.05),
        "rel_bias": nrm(ks[8], (NUM_BUCKETS, ATTN_HEADS), 0.5),
        "pool_w": nrm(ks[9], (L, POOL_GROUPS, POOL_GROUP_W, POOL_GROUP_W), POOL_GROUP_W ** -0.5),
        "pool_scale": 1.0 + nrm(ks[10], (L, POOL_W), 0.1),
        "w_branch_attn": nrm(ks[11], (L, ATTN_W, D), ATTN_W ** -0.5),
        "w_branch_pool": nrm(ks[12], (L, POOL_W, D), POOL_W ** -0.5),
        "w_out": nrm(ks[13], (L, D, D), D ** -0.5),
        "norm2_g": 1.0 + nrm(ks[14], (L, D), 0.05),
        "w_up": nrm(ks[15], (L, D, 2 * D_FF), D ** -0.5),
        "conv_w": nrm(ks[16], (L, CONV_W, 2 * D_FF), CONV_W ** -0.5),
        "conv_b": nrm(ks[17], (L, 2 * D_FF), 0.02),
        "w_down": nrm(ks[18], (L, D_FF, D), D_FF ** -0.5),
    }


def reference(x, c, ada_w, ada_b, norm1_g, w_in, q_norm_g, k_norm_g, rel_bias, pool_w,
              pool_scale, w_branch_attn, w_branch_pool, w_out, norm2_g, w_up, conv_w, conv_b,
              w_down):
    B, S, D = x.shape
    split_at = [ATTN_W, 2 * ATTN_W, 3 * ATTN_W, 3 * ATTN_W + POOL_W, 3 * ATTN_W + POOL_W + D_MODEL]
    c_act = jax.nn.silu(c)
    for l in range(DEPTH):
        mod = c_act @ ada_w[l] + ada_b[l]
        shift1, scale1, gate1, shift2, scale2, gate2 = [m[:, None, :] for m in jnp.split(mod, 6, axis=-1)]

        h = rms_norm(x, norm1_g[l]) * (1 + scale1) + shift1
        z = h @ w_in[l]
        q, k, v, p, g_attn, g_pool = jnp.split(z, split_at, axis=-1)
        heads = lambda t: t.reshape(B, S, ATTN_HEADS, HEAD_DIM).transpose(0, 2, 1, 3)
        q = rms_norm(heads(q), q_norm_g[l])
        k = rms_norm(heads(k), k_norm_g[l])
        attn = moba_attention(q, k, heads(v), rel_bias)
        attn = attn.transpose(0, 2, 1, 3).reshape(B, S, ATTN_W)
        pool = multiscale_pool(p.reshape(B, S, POOL_GROUPS, POOL_GROUP_W))
        pool = jnp.einsum('bsgc,gcd->bsgd', pool, pool_w[l]).reshape(B, S, POOL_W) * pool_scale[l]
        merged = (jax.nn.sigmoid(g_attn) * (attn @ w_branch_attn[l])
                  + jax.nn.sigmoid(g_pool) * (pool @ w_branch_pool[l]))
        x = x + gate1 * (merged @ w_out[l])

        h2 = rms_norm(x, norm2_g[l]) * (1 + scale2) + shift2
        u = causal_dwconv(h2 @ w_up[l], conv_w[l], conv_b[l])
        u_g, u_v = jnp.split(u, 2, axis=-1)
        x = x + gate2 * ((jax.nn.silu(u_g) * u_v) @ w_down[l])
    return x
```

```python
import numpy as np
import ml_dtypes
from contextlib import ExitStack
import concourse.bass as bass
import concourse.mybir as mybir
from concourse.bass_utils import run_bass_kernel_spmd

F32 = mybir.dt.float32
BF16 = mybir.dt.bfloat16
AF = mybir.ActivationFunctionType
ALU = mybir.AluOpType
AX = mybir.AxisListType

ENGS = ['sync', 'scalar', 'vector', 'gpsimd', 'tensor']
NEG = -30000.0
EPS = 1e-6
NOWN = 17
SCHED_PHASES = [True, True, True, True, True, True]
QT = NOWN * 128


class Op:
    __slots__ = ('eng', 'fn', 'deps', 'flag', 'val', 'sem', 'is_dma', 'dur', 'fin', 'mode', 'force', 'slack', 'bg')


class Prog:
    def __init__(self, nc, stack):
        self.nc = nc
        self.stack = stack
        self.esem = {}
        self.ecount = {}
        for e in ['scalar', 'vector', 'gpsimd', 'tensor']:
            self.esem[e] = stack.enter_context(nc.semaphore('es_' + e))
            self.ecount[e] = 0
        self.dsem = {}
        self.state = {}
        self.ops = []
        self.waited = {e: {} for e in ENGS}
        self.phase_idx = 0
        self.cur_slack = 0.0
        self.extra = None

    def _conf(self, key):
        name, slot = key if isinstance(key, tuple) else (key, None)
        st = self.state.setdefault(name, {})
        if slot is None:
            return name, slot, list(st.keys())
        ks = []
        if slot in st:
            ks.append(slot)
        if None in st:
            ks.append(None)
        return name, slot, ks

    def op(self, eng, fn, reads=(), writes=(), dma=None, dur=0.3, mode=None):
        o = Op()
        o.dur = dur
        o.fin = None
        o.mode = mode
        o.force = None
        o.slack = self.cur_slack
        o.bg = False
        o.eng = eng
        o.fn = fn
        o.deps = []
        o.flag = False
        o.val = None
        o.sem = None
        o.is_dma = dma is not None
        deps = o.deps
        if self.extra is not None:
            deps.append(self.extra)
        for key in reads:
            name, slot, ks = self._conf(key)
            st = self.state[name]
            for k in ks:
                w = st[k][0]
                if w is not None:
                    deps.append(w)
        for key in writes:
            name, slot, ks = self._conf(key)
            st = self.state[name]
            for k in ks:
                w, rd = st[k]
                if w is not None:
                    deps.append(w)
                deps.extend(rd)
        for key in reads:
            name, slot = key if isinstance(key, tuple) else (key, None)
            st = self.state[name]
            if slot not in st:
                st[slot] = [None, []]
            st[slot][1].append(o)
        for key in writes:
            name, slot = key if isinstance(key, tuple) else (key, None)
            st = self.state[name]
            if slot is None:
                st.clear()
            st[slot] = [o, []]
        if dma is not None:
            if dma not in self.dsem:
                self.dsem[dma] = [self.stack.enter_context(self.nc.semaphore('ds_' + dma)), 0, None]
            d = self.dsem[dma]
            if d[2] is not None:
                deps.append(d[2])
            d[1] += 16
            d[2] = o
            o.sem = d[0]
            o.val = d[1]
        self.ops.append(o)
        self.last = o
        return o

    def schedule(self, W=48, LAT=0.35):
        pend = {e: [] for e in ENGS}
        for o in self.ops:
            pend[o.eng].append(o)
        ptr = {e: 0 for e in ENGS}
        win = {e: [] for e in ENGS}
        tnow = {e: 0.0 for e in ENGS}
        pemode = [None]
        out = []
        remaining = len(self.ops)
        while remaining:
            best = None
            for e in ENGS:
                w = win[e]
                lst = pend[e]
                while len(w) < W and ptr[e] < len(lst):
                    w.append(lst[ptr[e]])
                    ptr[e] += 1
                te = tnow[e]
                for o in w:
                    ready = te
                    ok = True
                    sl = o.slack
                    for d in o.deps:
                        f = d.fin
                        if f is None:
                            ok = False
                            break
                        f += (LAT if (d.eng != e or d.is_dma) else 0.15) + sl
                        if f > ready:
                            ready = f
                    if ok and e == 'tensor' and o.mode != pemode[0]:
                        ready = max(ready, te + 0.4)
                    if ok and o.bg:
                        ready = max(ready, te) + 0.25
                    if ok and (best is None or ready < best[0] - 1e-9):
                        best = (ready, e, o)
                    if ok and ready <= te:
                        break
            ready, e, o = best
            win[e].remove(o)
            if e == 'tensor':
                pemode[0] = o.mode
            if o.is_dma:
                o.fin = ready + o.dur
                tnow[e] = ready + 0.06
            else:
                o.fin = ready + o.dur
                tnow[e] = o.fin
            out.append(o)
            remaining -= 1
        self.ops = out
        self.sim_time = max(tnow.values())

    def emit(self, final=False):
        nc = self.nc
        if SCHED_PHASES[self.phase_idx]:
            self.schedule()
        self.phase_idx += 1
        prev = None
        for o in self.ops:
            if o.eng == 'tensor' and not o.is_dma:
                if prev is not None and prev.mode != o.mode:
                    o.force = prev
                    prev.flag = True
                prev = o
        for o in self.ops:
            for d in o.deps:
                if not d.is_dma:
                    if d.eng == 'tensor' and o.eng == 'tensor' and not o.is_dma:
                        continue
                    d.flag = True
        for o in self.ops:
            if not o.is_dma and o.flag:
                self.ecount[o.eng] += 1
                o.sem = self.esem[o.eng]
                o.val = self.ecount[o.eng]
        per = {e: [] for e in ENGS}
        for o in self.ops:
            per[o.eng].append(o)
        dma_final = [(d[0], d[1]) for d in self.dsem.values() if d[1] > 0]

        def replay(eng_name):
            def body(e):
                wt = self.waited[eng_name]
                for o in per[eng_name]:
                    need = {}
                    for d in o.deps:
                        if d.val is None:
                            continue
                        if (not d.is_dma) and d.eng == 'tensor' and eng_name == 'tensor' and not o.is_dma:
                            continue
                        k = id(d.sem)
                        if k not in need or need[k][1] < d.val:
                            need[k] = (d.sem, d.val)
                    if o.force is not None:
                        d = o.force
                        k = id(d.sem)
                        if k not in need or need[k][1] < d.val:
                            need[k] = (d.sem, d.val)
                    for k, (sem, val) in need.items():
                        if wt.get(k, 0) >= val:
                            continue
                        e.wait_ge(sem, val)
                        wt[k] = val
                    ins = o.fn(e)
                    if o.is_dma:
                        ins.then_inc(o.sem, 16)
                    elif o.flag:
                        ins.then_inc(o.sem, 1)
                if eng_name in ('sync', 'gpsimd') or final:
                    for sem, val in dma_final:
                        k = id(sem)
                        if wt.get(k, 0) >= val:
                            continue
                        e.wait_ge(sem, val)
                        wt[k] = val
            return body

        with nc.Block() as block:
            block.sync(replay('sync'))
            block.scalar(replay('scalar'))
            block.vector(replay('vector'))
            block.gpsimd(replay('gpsimd'))
            block.tensor(replay('tensor'))
        self.ops = []
        self.state = {}
        for d in self.dsem.values():
            d[2] = None


def build_nc():
    nc = bass.Bass("TRN2", target_bir_lowering=False)
    din = lambda n, s, d=F32: nc.dram_tensor(n, list(s), d, kind="ExternalInput").ap()
    xw = din("xw", [8192, 1024])
    cT_d = din("cT", [128, 8])
    adaw_d = din("ada_w", [1024, 6144])
    adab_d = din("adab_rep", [128, 6144])
    g1rep_d = din("g1rep", [128, 1024])
    g2rep_d = din("g2rep", [128, 1024])
    win_d = din("w_in", [1024, 4096])
    gkcol_d = din("gkcol", [128, 1])
    gqcol_d = din("gqcol", [128, 1])
    traw_d = din("traw", [128, 8, 640])
    cmask_d = din("cmask", [128, 640])
    rb31_d = din("rb31", [128, 8])
    onehot_d = din("onehot", [32, 8192], BF16)
    validb_d = din("validb17", [128, 17, 32])
    notown_d = din("notown17", [128, 17, 32], BF16)
    amat_d = din("amats", [128, 16, 128], BF16)
    hflag_d = din("hflag", [128, 1])
    poolw_d = din("poolw", [128, 4, 128])
    pscale_d = din("pscale", [128, 4])
    wba_d = din("wba", [512, 1024])
    wbp_d = din("wbp", [512, 1024])
    wout_d = din("wout", [1024, 1024])
    wup_d = din("wup", [1024, 5632])
    convw_d = din("convw", [128, 3, 44])
    convb_d = din("convb", [128, 44])
    wdown_d = din("wdown", [2816, 1024])
    shiftm_d = din("shiftm", [128, 64])
    out_d = nc.dram_tensor("out", [2048, 1024], F32, kind="ExternalOutput").ap()
    KS = nc.dram_tensor("KS", [8, 64, 8192], BF16, kind="Internal").ap()
    VS = nc.dram_tensor("VS", [8, 128, 64 * 64], BF16, kind="Internal").ap()
    QS = nc.dram_tensor("QS", [8, 64, QT], BF16, kind="Internal").ap()
    SGS = nc.dram_tensor("SGS", [16, 128, QT], BF16, kind="Internal").ap()
    X1S = nc.dram_tensor("X1S", [NOWN, 128, 1024], F32, kind="Internal").ap()
    THS = nc.dram_tensor("THS", [128, 8, 640], BF16, kind="Internal").ap()
    TLS = nc.dram_tensor("TLS", [128, 8, 640], BF16, kind="Internal").ap()

    ARENA_KB = 206
    with ExitStack() as st:
        arena = st.enter_context(nc.sbuf_tensor("arena", [128, ARENA_KB * 256], F32))
        psum = st.enter_context(nc.psum_tensor("psum", [128, 4096], F32))
        a32 = arena
        a16 = arena.bitcast(BF16)
        ps16 = psum.bitcast(BF16)
        P = Prog(nc, st)

        class Alloc:
            def __init__(self, base_kb, limit_kb):
                self.off = int(base_kb * 1024)
                self.limit = int(limit_kb * 1024)

            def __call__(self, shape, dt, parts=128):
                n = 1
                for s in shape:
                    n *= s
                esz = 4 if dt == F32 else 2
                nb = (n * esz + 63) // 64 * 64
                off = self.off
                self.off += nb
                assert self.off <= self.limit, (self.off, self.limit)
                base = a32 if dt == F32 else a16
                v = base[0:parts, off // esz: off // esz + n]
                if len(shape) == 2:
                    v = v.rearrange("p (a b) -> p a b", b=shape[1])
                elif len(shape) == 3:
                    v = v.rearrange("p (a b c) -> p a b c", b=shape[1], c=shape[2])
                return v

        def bank(i, n=512, parts=128, off=0):
            return psum[0:parts, i * 512 + off: i * 512 + off + n]

        def bank16(i, n=1024, parts=128):
            return ps16[0:parts, i * 1024: i * 1024 + n]

        def fs(ap):
            n = 1
            for d in ap.shape[1:]:
                n *= d
            return n

        def rk(n):
            return 32 if n <= 32 else (64 if n <= 64 else 128)

        def edur(eng, n):
            if eng == 'scalar':
                return 0.15 + n / 1150.0
            if eng == 'vector':
                return 0.16 + n / 950.0
            return 0.25 + n / 450.0

        def MM(out, lhsT, rhs, start, stop, R, W):
            d = max(0.07, 0.01 + fs(rhs) / 2350.0)
            if lhsT.dtype == F32:
                d *= 4
            P.op('tensor', lambda e: e.matmul(out, lhsT=lhsT, rhs=rhs, start=start, stop=stop), R, W, dur=d,
                 mode=(rk(lhsT.shape[0]), rk(fs(lhsT)), lhsT.dtype == F32, False))

        def TR(out, in_, ident, R, W):
            P.op('tensor', lambda e: e.transpose(out, in_, ident), R, W, dur=(0.42 if in_.dtype == F32 else 0.12),
                 mode=(rk(in_.shape[0]), rk(fs(in_)), in_.dtype == F32, True))

        def ACT(out, in_, func, R, W, **kw):
            P.op('scalar', lambda e: e.activation(out=out, in_=in_, func=func, **kw), R, W,
                 dur=edur('scalar', fs(in_)) + (0.1 if 'accum_out' in kw else 0.0))

        def DMA(q, out, in_, R, W, sem):
            nb = fs(out) * out.shape[0] * (4 if out.dtype == F32 else 2)
            P.op(q, lambda e: e.dma_start(out=out, in_=in_), R, W, dma=sem, dur=2.0 + nb / 120000.0)

        def TT(eng, out, in0, in1, op, R, W):
            P.op(eng, lambda e: e.tensor_tensor(out=out, in0=in0, in1=in1, op=op), R, W, dur=edur(eng, fs(out)))

        def TS(eng, out, in0, s1, s2, op0, op1, R, W):
            if op1 is None:
                P.op(eng, lambda e: e.tensor_scalar(out=out, in0=in0, scalar1=s1, scalar2=None, op0=op0), R, W,
                     dur=edur(eng, fs(out)))
            else:
                P.op(eng, lambda e: e.tensor_scalar(out=out, in0=in0, scalar1=s1, scalar2=s2, op0=op0, op1=op1), R, W,
                     dur=edur(eng, fs(out)))

        def STT(out, in0, scalar, in1, op0, op1, R, W):
            P.op('vector', lambda e: e.scalar_tensor_tensor(out=out, in0=in0, scalar=scalar, in1=in1, op0=op0, op1=op1), R, W,
                 dur=edur('vector', fs(out)))

        def CP(eng, out, in_, R, W):
            if eng == 'scalar':
                P.op('scalar', lambda e: e.copy(out, in_), R, W, dur=edur(eng, fs(out)))
            else:
                P.op(eng, lambda e: e.tensor_copy(out, in_), R, W, dur=edur(eng, fs(out)))

        def MEMSET(eng, ap, val, W):
            P.op(eng, lambda e: e.memset(ap, val), (), W, dur=edur(eng, fs(ap)))

        def RECIP(out, in_, R, W):
            P.op('vector', lambda e: e.reciprocal(out, in_), R, W, dur=0.08 + 8 * fs(out) / 900.0)

        crr = [0]

        def CDMA(out, in_, W, cast=False):
            crr[0] += 1
            q = 'gpsimd' if cast else 'sync'
            DMA(q, out, in_, (), W, ('cg%d' if cast else 'cs%d') % (crr[0] % 4))

        G = Alloc(0, 34)
        modrep = G([6144], F32)
        G1rep = G([1024], F32)
        G2rep = G([1024], F32)
        ident_bf = G([128], BF16)
        ident_f = G([128], F32)
        gkcol = G([1], F32)
        gqcol = G([1], F32)
        gk256 = G([1], F32)
        hflag = G([1], F32)
        onesf = G([2], F32)
        shift1rep = modrep[:, 0:1024]
        gate1rep = modrep[:, 2048:3072]
        shift2rep = modrep[:, 3072:4096]
        gate2rep = modrep[:, 5120:6144]

        p_own = Alloc(34, 51)([NOWN, 512], BF16)
        attnT = Alloc(51, 85)([8, QT], BF16, parts=64)
        h2T = Alloc(172, 206)([8, QT], BF16)
        kmeanT = G([8, 32], BF16)

        T0 = Alloc(51, 200)
        cT = T0([8], F32)
        cact = T0([8], F32)
        cbc = T0([8, 128], F32)
        g1r = T0([1024], F32)
        g2r = T0([1024], F32)
        adab = T0([6144], F32)
        adaw = [T0([3072], F32) for _ in range(3)]
        identtmp = T0([128], F32)

        CDMA(cT, cT_d, ['cT'])
        CDMA(adab, adab_d, ['adab'])
        CDMA(g1r, g1rep_d, ['g1r'])
        CDMA(g2r, g2rep_d, ['g2r'])
        CDMA(gkcol, gkcol_d, ['gkcol'])
        CDMA(gqcol, gqcol_d, ['gqcol'])
        CDMA(hflag, hflag_d, ['hflag'])
        MEMSET('gpsimd', ident_f, 1.0, ['ident_f'])
        P.op('gpsimd', lambda e: e.affine_select(out=ident_f, in_=ident_f, pattern=[[-1, 128]],
                                                 compare_op=ALU.is_equal, fill=0.0, base=0, channel_multiplier=1),
             ['ident_f'], ['ident_f'])
        CP('vector', ident_bf, ident_f, ['ident_f'], ['ident_bf'])
        MEMSET('vector', onesf, 1.0, ['onesf'])
        onesb = G([2], BF16)
        MEMSET('vector', onesb, 1.0, ['onesb'])
        epsb = G([1], F32)
        MEMSET('vector', epsb, EPS, ['epsb'])
        ACT(cact, cT, AF.Silu, ['cT'], ['cact'])
        CP('vector', cbc, cact.unsqueeze(2).to_broadcast([128, 8, 128]), ['cact'], ['cbc'])
        P.op('scalar', lambda e: e.mul(gk256, gkcol, 1.0 / 256.0), ['gkcol'], ['gk256'])
        P.op('scalar', lambda e: e.mul(gqcol, gqcol, 0.125), ['gqcol'], ['gqcol'])
        adaw_v = adaw_d.rearrange("(k p) c -> k p c", p=128)
        cnt = 0
        for hf in range(2):
            for kt in range(8):
                s = cnt % 3
                cnt += 1
                DMA('sync', adaw[s], adaw_v[kt, :, hf * 3072:(hf + 1) * 3072], (), [('adaw', s)], 'adaw%d' % s)
                for c in range(6):
                    MM(bank(c), cbc[:, kt, :], adaw[s][:, c * 512:(c + 1) * 512], kt == 0, kt == 7,
                       [('adaw', s), 'cbc'], [('ps', c)])
            for c in range(6):
                col = hf * 3072 + c * 512
                TT('vector', modrep[:, col:col + 512], bank(c), adab[:, col:col + 512], ALU.add,
                   [('ps', c), 'adab'], [('modrep', col // 1024)])
        Thi0 = T0([8, 640], BF16)
        Tlo0 = T0([8, 640], BF16)
        tA0 = T0([640], F32)
        tB0 = T0([640], F32)
        cmask0 = T0([640], F32)
        rb310 = T0([8], F32)
        CDMA(cmask0, cmask_d, ['cmask0'])
        CDMA(rb310, rb31_d, ['rb310'])
        for h in range(8):
            DMA('sync', tA0, traw_d[:, h, :], (), ['tA0'], 'tra')
            STT(tA0, tA0, rb310[:, h:h + 1], cmask0, ALU.subtract, ALU.add, ['tA0', 'rb310', 'cmask0'], ['tA0'])
            CP('vector', Thi0[:, h, :], tA0, ['tA0'], [('Thi0', h)])
            TT('vector', tB0, tA0, Thi0[:, h, :], ALU.subtract, ['tA0', ('Thi0', h)], ['tB0'])
            CP('vector', Tlo0[:, h, :], tB0, ['tB0'], [('Tlo0', h)])
        DMA('sync', THS, Thi0, ['Thi0'], ['THS'], 'ths')
        DMA('sync', TLS, Tlo0, ['Tlo0'], ['TLS'], 'ths')
        STT(G1rep, modrep[:, 1024:2048], 1.0, g1r, ALU.add, ALU.mult, [('modrep', 1), 'g1r'], ['G1rep'])
        STT(G2rep, modrep[:, 4096:5120], 1.0, g2r, ALU.add, ALU.mult, [('modrep', 4), 'g2r'], ['G2rep'])
        P.emit()

        T1 = Alloc(51, 206)
        Wkv = T1([8, 1024], BF16)
        Wqp = T1([8, 1024], BF16)
        Wg = T1([8, 2048], BF16)
        xt = [T1([1024], F32) for _ in range(2)]
        tmp = T1([1024], F32)
        hb = [T1([1024], BF16) for _ in range(2)]
        hTs = [T1([8, 128], BF16) for _ in range(2)]
        hTg = [T1([8, 512], BF16) for _ in range(2)]
        sq = T1([512], F32)
        knp = [T1([512], BF16) for _ in range(2)]
        qnp = [T1([512], BF16) for _ in range(2)]
        Kst = [T1([4, 512], BF16) for _ in range(2)]
        Vst = [T1([8, 4 * 64], BF16) for _ in range(2)]
        Qst = T1([4, 512], BF16)
        sgst = T1([8, 512], BF16)
        junk = T1([1024], BF16)
        small = T1([96], F32)

        win_v = win_d.rearrange("(k p) c -> p k c", p=128)
        CDMA(Wkv, win_v[:, :, 512:1536], ['Wkv'], cast=True)

        def late_w_loads():
            P.extra = P.last
            CDMA(Wqp[:, :, 0:512], win_v[:, :, 0:512], [('Wqp', 0)], cast=True)
            CDMA(Wqp[:, :, 512:1024], win_v[:, :, 1536:2048], [('Wqp', 1)], cast=True)
            CDMA(Wg[:, :, 0:1024], win_v[:, :, 2048:3072], [('Wg', 0)], cast=True)
            CDMA(Wg[:, :, 1024:2048], win_v[:, :, 3072:4096], [('Wg', 1)], cast=True)
            P.extra = None

        xw_v = xw.rearrange("(t p) c -> t p c", p=128)
        KS_v = KS.rearrange("(hp two) d t -> (two d) hp t", two=2)
        QS_v = QS.rearrange("(hp two) d t -> (two d) hp t", two=2)
        VS_v = VS.rearrange("h p x -> p h x")
        SGS_v = SGS.rearrange("g p t -> p g t")
        zrr = [0]

        def zbank():
            zrr[0] += 1
            return 1 + (zrr[0] % 3)

        def hT_of(wt):
            ti = wt - 47
            if ti <= 0:
                return hTs[wt % 2], ('hTs', wt % 2)
            tl = (ti - 1) % 4
            gp = ((ti - 1) // 4) % 2
            return hTg[gp][:, :, tl * 128:(tl + 1) * 128], ('hTg%d' % gp, tl)

        def stageA(wt):
            s = wt % 2
            sm = small[:, (wt % 8) * 4:(wt % 8) * 4 + 4]
            smk = ('small', wt % 8)
            DMA('sync', xt[s], xw_v[wt], (), [('xt', s)], 'xt%d' % s)
            ACT(junk, xt[s], AF.Square, [('xt', s)], ['junk', smk], accum_out=sm[:, 0:1])
            ACT(sm[:, 1:2], sm[:, 0:1], AF.Sqrt, [smk], [smk], scale=1.0 / 1024.0, bias=epsb)
            RECIP(sm[:, 2:3], sm[:, 1:2], [smk], [smk])
            STT(tmp, xt[s], sm[:, 2:3], G1rep, ALU.mult, ALU.mult, [('xt', s), smk, 'G1rep'], ['tmp'])
            TT('vector', hb[s], tmp, shift1rep, ALU.add, ['tmp', ('modrep', 0)], [('hb', s)])
            tpb = 0 if s == 0 else 7
            for k in range(8):
                TR(bank16(tpb)[:, k * 128:(k + 1) * 128], hb[s][:, k * 128:(k + 1) * 128], ident_bf,
                   [('hb', s), 'ident_bf'], [('ps', tpb)])
            hv, hk = hT_of(wt)
            CP('scalar', hv, bank16(tpb).rearrange("p (k t) -> p k t", t=128), [('ps', tpb)], [hk])

        def headnorm(zb, dst, dkey, smk, sm):
            ACT(sq, bank(zb), AF.Square, [('ps', zb)], ['sq'])
            P.op('vector', lambda e: e.tensor_reduce(out=sm, in_=sq.rearrange("p (h d) -> p h d", d=64),
                                                      axis=AX.X, op=ALU.add), ['sq'], [smk])
            ACT(sm, sm, AF.Sqrt, [smk], [smk], scale=1.0 / 64.0, bias=epsb)
            RECIP(sm, sm, [smk], [smk])
            TT('vector', dst.rearrange("p (h d) -> p h d", d=64), bank(zb).rearrange("p (h d) -> p h d", d=64),
               sm.unsqueeze(2).to_broadcast([128, 8, 64]), ALU.mult, [('ps', zb), smk], [dkey])

        def stageB(wt):
            ti = wt - 47
            hv, hk = hT_of(wt)
            s = wt % 2
            zk = zbank()
            for k in range(8):
                MM(bank(zk), hv[:, k, :], Wkv[:, k, 0:512], k == 0, k == 7, [hk, 'Wkv'], [('ps', zk)])
            zv = zbank()
            for k in range(8):
                MM(bank(zv), hv[:, k, :], Wkv[:, k, 512:1024], k == 0, k == 7, [hk, 'Wkv'], [('ps', zv)])
            smk = ('smallk', wt % 4)
            sm = small[:, 48 + (wt % 4) * 8: 48 + (wt % 4) * 8 + 8]
            headnorm(zk, knp[s], ('knp', s), smk, sm)
            gv, tlv = (wt // 4) % 2, wt % 4
            CP('scalar', Vst[gv].rearrange("p h (t d) -> p h t d", d=64)[:, :, tlv, :],
               bank(zv).rearrange("p (h d) -> p h d", d=64), [('ps', zv)], [('Vst', gv)])
            if tlv == 3:
                t0 = wt - 3
                DMA('sync', VS_v[:, :, t0 * 64:(t0 + 4) * 64], Vst[gv], [('Vst', gv)], [('VS', wt // 4)], 'vst%d' % gv)
            if ti >= 0:
                zq = zbank()
                for k in range(8):
                    MM(bank(zq), hv[:, k, :], Wqp[:, k, 0:512], k == 0, k == 7, [hk, ('Wqp', 0)], [('ps', zq)])
                zp = zbank()
                for k in range(8):
                    MM(bank(zp), hv[:, k, :], Wqp[:, k, 512:1024], k == 0, k == 7, [hk, ('Wqp', 1)], [('ps', zp)])
                smq = ('smallq', wt % 2)
                smqv = small[:, 32 + (wt % 2) * 8: 32 + (wt % 2) * 8 + 8]
                headnorm(zq, qnp[s], ('qnp', s), smq, smqv)
                CP('scalar', p_own[:, ti, :], bank(zp), [('ps', zp)], [('p_own', ti)])

        def stageC(wt):
            ti = wt - 47
            s = wt % 2
            for hp in range(4):
                TR(bank16(4)[:, hp * 128:(hp + 1) * 128], knp[s][:, hp * 128:(hp + 1) * 128], ident_bf,
                   [('knp', s), 'ident_bf'], [('ps', 4)])
            for hp in range(4):
                MM(psum[:, 6 * 512 + hp * 64 + wt: 6 * 512 + hp * 64 + wt + 1], knp[s][:, hp * 128:(hp + 1) * 128],
                   onesb[:, 0:1], True, True, [('knp', s), 'onesb'], [('ps', 6)])
            gk_, tlk = (wt // 4) % 2, wt % 4
            ACT(Kst[gk_][:, :, tlk * 128:(tlk + 1) * 128],
                bank16(4, 512).rearrange("p (h t) -> p h t", t=128), AF.Copy,
                [('ps', 4), 'gkcol'], [('Kst', gk_)], scale=gkcol)
            if tlk == 3:
                t0 = wt - 3
                DMA('sync', KS_v[:, :, t0 * 128:(t0 + 4) * 128], Kst[gk_], [('Kst', gk_)], [('KS', wt // 4)], 'kst%d' % gk_)
            if ti >= 0:
                for hp in range(4):
                    TR(bank16(5)[:, hp * 128:(hp + 1) * 128], qnp[s][:, hp * 128:(hp + 1) * 128], ident_bf,
                       [('qnp', s), 'ident_bf'], [('ps', 5)])
                tl = 0 if ti == 0 else (ti - 1) % 4
                ACT(Qst[:, :, tl * 128:(tl + 1) * 128],
                    bank16(5, 512).rearrange("p (h t) -> p h t", t=128), AF.Copy,
                    [('ps', 5), 'gqcol'], ['Qst'], scale=gqcol)
                if ti == 0 or tl == 3:
                    n = 128 if ti == 0 else 512
                    tok0 = 0 if ti == 0 else (ti - 3) * 128
                    DMA('sync', QS_v[:, :, tok0:tok0 + n], Qst[:, :, 0:n], ['Qst'], [('QS', ti)], 'qst')
                    if ti == 0:
                        hgv, hgk = hTs[wt % 2], [('hTs', wt % 2)]
                    else:
                        gp = ((ti - 1) // 4) % 2
                        hgv, hgk = hTg[gp], ['hTg%d' % gp]
                    for gt in range(16):
                        zg = zbank()
                        for k in range(8):
                            MM(bank(zg, n), Wg[:, k, gt * 128:(gt + 1) * 128], hgv[:, k, 0:n], k == 0, k == 7,
                               hgk + [('Wg', gt // 8)], [('ps', zg)])
                        ACT(sgst[:, gt % 8, 0:n], bank(zg, n), AF.Sigmoid, [('ps', zg)], [('sgst', gt % 8)])
                        if gt % 8 == 7:
                            g0 = gt - 7
                            DMA('sync', SGS_v[:, g0:g0 + 8, tok0:tok0 + n], sgst[:, :, 0:n], ['sgst'], [('SGS', ti)], 'sgst')

        stageA(0)
        for wt in range(64):
            if wt == 6:
                late_w_loads()
            if wt + 1 < 64:
                stageA(wt + 1)
            stageB(wt)
            if wt >= 1:
                stageC(wt - 1)
        stageC(63)
        kms = T1([4, 32], F32)
        kmp = T1([4, 32], BF16)
        ksv = psum[:, 6 * 512: 6 * 512 + 256].rearrange("p (h b two) -> p h b two", h=4, two=2)
        CP('scalar', kms, ksv[:, :, :, 0], [('ps', 6)], ['kms'])
        TT('vector', kms, kms, ksv[:, :, :, 1], ALU.add, ['kms', ('ps', 6)], ['kms'])
        ACT(kmp, kms, AF.Copy, ['kms', 'gk256'], ['kmp'], scale=gk256)
        kmT4 = kmeanT[0:64].rearrange("p (hp two) b -> p hp two b", two=2)
        DMA('sync', kmT4[:, :, 0, :], kmp[0:64], ['kmp'], [('kmeanT', 0)], 'kmt')
        DMA('sync', kmT4[:, :, 1, :], kmp[64:128], ['kmp'], [('kmeanT', 1)], 'kmt')
        P.emit()

        T2 = Alloc(85, 206)
        Kaug = [T2([8192], BF16, parts=96) for _ in range(2)]
        Vaug = [T2([64, 128], BF16) for _ in range(2)]
        Vld = T2([64 * 64], BF16)
        Qaug = [T2([QT], BF16, parts=96) for _ in range(2)]
        Thi = T2([8, 640], BF16)
        Tlo = T2([8, 640], BF16)
        cmask = T2([640], F32)
        rb31 = T2([8], F32)
        validb = T2([17, 32], F32)
        notown = T2([17, 32], BF16)
        shiftm = T2([64], F32)
        gm = T2([NOWN], F32)
        sc_off = T2.off
        PT = [T2([4, 256], BF16) for _ in range(3)]
        tmpA = T2([640], F32)
        tmpB = T2([640], F32)
        sc_end = T2.off
        GA_ = Alloc(sc_off / 1024.0, sc_end / 1024.0)
        gA = GA_([NOWN, 32], F32)
        gB = GA_([NOWN, 32], F32)
        gE = GA_([NOWN, 32], F32)
        Mf = GA_([NOWN, 96], BF16)
        SCK = [('PT', 0), ('PT', 1), ('PT', 2), 'tmpA', 'tmpB']
        Osb = [T2([256], F32) for _ in range(2)]

        CDMA(validb, validb_d, ['validb'])
        CDMA(notown, notown_d, ['notown'])
        CDMA(shiftm, shiftm_d, ['shiftm'])
        for i in range(2):
            CDMA(Kaug[i][64:96], onehot_d, [('Kaug1h', i)])
            MEMSET('gpsimd', Vaug[i][:, :, 64:128], 1.0, [('Vaug1', i)])

        srr = [0]
        prr = [0]
        orr = [0]
        mrr = [0]
        def p2_loads(h):
            hbuf = h % 2
            DMA('sync', Kaug[hbuf][0:64], KS[h], ['KS'], [('Kaug', hbuf)], 'ka%d' % hbuf)
            DMA('sync', Vld, VS[h], ['VS'], ['Vld'], 'vld')
            vl3 = Vld.rearrange("p (t d) -> p t d", d=64)
            for qd in range(4):
                CP('vector', Vaug[hbuf][:, qd * 16:(qd + 1) * 16, 0:64], vl3[:, qd * 16:(qd + 1) * 16, :],
                   ['Vld'], [('Vaug', hbuf)])
            DMA('sync', Qaug[hbuf][0:64], QS[h], ['QS'], [('Qaug', hbuf)], 'qa%d' % hbuf)

        def p2_gate_all(h):
            hbuf = h % 2
            nt = NOWN
            gps = psum[:, 0:nt * 32].rearrange("p (t n) -> p t n", n=32)
            gk = [('ps', 0), ('ps', 1)]
            for ti in range(nt):
                MM(gps[:, ti, :], Qaug[hbuf][0:64, ti * 128:(ti + 1) * 128], kmeanT[0:64, h, :], True, True,
                   [('Qaug', hbuf), 'kmeanT'], [('ps', ti // 16)])
            m = gm
            mb = m.unsqueeze(2).to_broadcast([128, nt, 32])

            def rmax(src):
                P.op('vector', lambda e: e.tensor_reduce(out=m, in_=src, axis=AX.X, op=ALU.max), SCK, ['gm'],
                     dur=0.15 + nt * 32 / 900.0)
            TT('vector', gA, gps, validb, ALU.add, gk + ['validb'], SCK)
            rmax(gA)
            TT('vector', gE, gA, mb, ALU.is_equal, ['gm'], SCK)
            STT(gB, gE, -2e30, gA, ALU.mult, ALU.add, [], SCK)
            rmax(gB)
            TT('vector', gE, gB, mb, ALU.is_equal, ['gm'], SCK)
            STT(gB, gE, -2e30, gB, ALU.mult, ALU.add, [], SCK)
            rmax(gB)
            TS('vector', m, m, -1e29, None, ALU.max, None, ['gm'], ['gm'])
            TT('vector', gE, gA, mb, ALU.is_ge, ['gm'], SCK)
            TS('vector', gE, gE, 1.0, -NEG, ALU.subtract, ALU.mult, [], SCK)
            TT('vector', Mf[:, :, 64:96], gE, notown, ALU.mult, ['notown'], SCK)
            for ti in range(nt):
                bk = 2 + ti // 8
                TR(ps16[0:96, bk * 1024 + (ti % 8) * 128: bk * 1024 + (ti % 8 + 1) * 128], Mf[:, ti, :], ident_bf,
                   SCK + ['ident_bf'], [('ps', bk)])
            for bk in range(2, 5):
                t_lo, t_hi = (bk - 2) * 8, min(nt, (bk - 1) * 8)
                if t_lo >= nt:
                    continue
                n_ = (t_hi - t_lo) * 128
                CP('vector', Qaug[hbuf][64:96, t_lo * 128: t_hi * 128], ps16[64:96, bk * 1024: bk * 1024 + n_],
                   [('ps', bk)], [('Qmask', hbuf)])

        GBATCH = [(0, 1), (1, 4), (5, 4), (9, 4), (13, 4)]

        pending_fin = []

        def p2_chunk(h, ci, hooks):
            hbuf = h % 2
            if ci == 0:
                b, nq, q0, qc0 = 23, 128, 128, 0
            else:
                b, nq, q0, qc0 = 23 + ci, 256, 0, (2 * ci - 1) * 128
            oi = orr[0] % 2
            orr[0] += 1
            ob = 6
            Ops = bank(ob, nq)
            Kk = [('Kaug', hbuf), ('Kaug1h', hbuf)]
            far = list(range(0, b - 1))
            fgroups = [far[i:i + 2] for i in range(0, len(far), 2)]
            groups = [(fgroups[0], False), ([b - 1, b], True)] + [(g, False) for g in fgroups[1:]]
            npv = sum(2 * len(g) for g, _ in groups)
            pvc = 0
            for gi, (grp, near) in enumerate(groups):
                sp = srr[0] % 3
                srr[0] += 1
                sv = psum[:, sp * 1024:(sp + 1) * 1024].rearrange("p (k q) -> p k q", q=256)
                skeys = [('ps', 2 * sp + i) for i in range(len(grp))]
                for i, n in enumerate(grp):
                    for ktl in range(2):
                        kt_ = 2 * n + ktl
                        MM(sv[:, 2 * i + ktl, 0:nq], Kaug[hbuf][0:96, kt_ * 128:(kt_ + 1) * 128],
                           Qaug[hbuf][0:96, qc0:qc0 + nq], True, not near,
                           Kk + [('Qaug', hbuf), ('Qmask', hbuf)], [('ps', 2 * sp + i)])
                        if near:
                            j0 = q0 + (128 if n == b else 384) - 128 * ktl
                            MM(sv[:, 2 * i + ktl, 0:nq], ident_bf, Thi[:, h, j0:j0 + nq], False, False,
                               ['ident_bf', ('Thi', h)], [('ps', 2 * sp + i)])
                            MM(sv[:, 2 * i + ktl, 0:nq], ident_bf, Tlo[:, h, j0:j0 + nq], False, True,
                               ['ident_bf', ('Tlo', h)], [('ps', 2 * sp + i)])
                pi = prr[0] % 3
                prr[0] += 1
                nk = 2 * len(grp)
                ACT(PT[pi][:, 0:nk, 0:nq], sv[:, 0:nk, 0:nq], AF.Exp, skeys, [('PT', pi)])
                for i, n in enumerate(grp):
                    for ktl in range(2):
                        kt_ = 2 * n + ktl
                        MM(Ops, Vaug[hbuf][:, kt_, :], PT[pi][:, 2 * i + ktl, 0:nq], pvc == 0,
                           pvc == npv - 1, [('Vaug', hbuf), ('Vaug1', hbuf), ('PT', pi)], [('ps', ob)])
                        pvc += 1
                if gi in hooks:
                    for fnc in hooks[gi]:
                        fnc(P.last)
            CP('vector', Osb[oi][:, 0:nq], Ops, [('ps', ob)], [('Osb', oi)])

            def fin1(anchor):
                RECIP(Osb[oi][64:128, 0:nq], Osb[oi][64:128, 0:nq], [('Osb', oi)], [('Osb', oi)])

            def fin2(anchor):
                shp = psum[0:64, 7 * 512: 7 * 512 + nq]
                P.extra = anchor
                P.cur_slack = 1.0
                MM(shp, shiftm, Osb[oi][:, 0:nq], True, True, [('Osb', oi), 'shiftm'], [('ps', 7)])
                P.extra = None
                TT('vector', attnT[0:64, h, qc0:qc0 + nq], Osb[oi][0:64, 0:nq], shp, ALU.mult,
                   [('Osb', oi), ('ps', 7)], [('attnT', h)])
                P.cur_slack = 0.0
            pending_fin.append((fin1, fin2))

        p2_loads(0)
        CDMA(Thi, THS, ['Thi'])
        CDMA(Tlo, TLS, ['Tlo'])
        p2_gate_all(0)
        p2_loads(1)
        for h in range(8):
            for ci in range(9):
                if ci == 1 and h >= 1 and h + 1 < 8:
                    p2_loads(h + 1)
                hooks = {}
                if pending_fin:
                    f1, f2 = pending_fin.pop(0)
                    hooks.setdefault(1, []).append(f1)
                    hooks.setdefault(5, []).append(f2)
                p2_chunk(h, ci, hooks)
            if h + 1 < 8:
                p2_gate_all(h + 1)
        while pending_fin:
            f1, f2 = pending_fin.pop(0)
            f1(None)
            f2(None)
        P.emit()

        T3 = Alloc(85, 172)
        Wba = T3([8, 1024], BF16, parts=64)
        Wbp = T3([4, 1024], BF16)
        poolw = T3([4, 128], BF16)
        Wout = T3([8, 1024], BF16)
        Am = T3([16, 128], BF16)
        pscale = T3([4], F32)
        sgc = [T3([2, 512], BF16) for _ in range(2)]
        pooledT = T3([4, 128], BF16)
        pool2T = [T3([4, 512], BF16)] * 2
        t1 = T3([512], BF16)
        t2 = T3([512], BF16)
        mergedT = [T3([8, 512], BF16) for _ in range(2)]
        xo = [T3([1024], F32), modrep[:, 4096:5120]]
        x1t = [T3([1024], F32), modrep[:, 0:1024]]
        tmp3 = [T3([1024], F32), modrep[:, 1024:2048]]
        h2b = [T3([1024], BF16), a16[:, 12288:12288 + 1024]]
        small3 = T3([16], F32)

        CDMA(Wba, wba_d.rearrange("(h d) c -> d h c", d=64), ['Wba'], cast=True)
        CDMA(Wbp, wbp_d.rearrange("(g p) c -> p g c", p=128), ['Wbp'], cast=True)
        CDMA(poolw, poolw_d, ['poolw'], cast=True)
        CDMA(Wout, wout_d.rearrange("(k p) c -> p k c", p=128), ['Wout'], cast=True)
        CDMA(Am, amat_d, ['Am'])
        CDMA(pscale, pscale_d, ['pscale'])

        chunks = [(0, 1)] + [(1 + 4 * c, 4) for c in range(4)]

        def st_a(cix, tl):
            tf, ntl = chunks[cix]
            ti = tf + tl
            pb_ = cix % 2
            cb, pb = (0, 4) if ti != 1 else (8, 12)
            for g in range(4):
                o = bank(0)[:, g * 128:(g + 1) * 128]
                MM(o, p_own[:, ti, g * 128:(g + 1) * 128], Am[:, cb + g, :], True, ti == 0, ['p_own', 'Am'], [('ps', 0)])
                if ti >= 1:
                    MM(o, p_own[:, ti - 1, g * 128:(g + 1) * 128], Am[:, pb + g, :], False, True, ['p_own', 'Am'], [('ps', 0)])
            CP('scalar', pooledT, bank(0).rearrange("p (g t) -> p g t", t=128), [('ps', 0)], ['pooledT'])
            for g in range(4):
                MM(bank(0)[:, g * 128:(g + 1) * 128], poolw[:, g, :], pooledT[:, g, :], True, True,
                   ['poolw', 'pooledT'], [('ps', 0)])
            for g in range(4):
                ACT(pool2T[pb_][:, g, tl * 128:(tl + 1) * 128], bank(0)[:, g * 128:(g + 1) * 128], AF.Copy,
                    [('ps', 0), 'pscale'], [('pool2T', tl)], scale=pscale[:, g:g + 1])

        def st_b(cix, mt):
            tf, ntl = chunks[cix]
            N, tok0, pb_ = ntl * 128, tf * 128, cix % 2
            si = mt % 2
            DMA('sync', sgc[si][:, 0, 0:N], SGS[mt, :, tok0:tok0 + N], (), [('sgc', si)], 'sgc%d' % si)
            DMA('sync', sgc[si][:, 1, 0:N], SGS[8 + mt, :, tok0:tok0 + N], (), [('sgc', si)], 'sgc%d' % si)
            ba, bp = 2 + (mt % 2), 4 + (mt % 2)
            for hh in range(8):
                MM(bank(ba, N), Wba[0:64, hh, mt * 128:(mt + 1) * 128], attnT[0:64, hh, tok0:tok0 + N],
                   hh == 0, hh == 7, ['Wba', 'attnT'], [('ps', ba)])
            for g in range(4):
                MM(bank(bp, N), Wbp[:, g, mt * 128:(mt + 1) * 128], pool2T[pb_][:, g, 0:N], g == 0, g == 3,
                   ['Wbp', 'pool2T'], [('ps', bp)])
            TT('vector', t1[:, 0:N], bank(ba, N), sgc[si][:, 0, 0:N], ALU.mult, [('ps', ba), ('sgc', si)], ['t1'])
            TT('vector', t2[:, 0:N], bank(bp, N), sgc[si][:, 1, 0:N], ALU.mult, [('ps', bp), ('sgc', si)], ['t2'])
            TT('vector', mergedT[pb_][:, mt, 0:N], t1[:, 0:N], t2[:, 0:N], ALU.add, ['t1', 't2'], [('mergedT%d' % pb_, mt)])

        def st_c(cix, tl):
            tf, ntl = chunks[cix]
            pb_ = cix % 2
            ti = tf + tl
            bi = ti % 2
            DMA('sync', xo[bi], xw_v[47 + ti], (), [('xo', bi)], 'xo%d' % bi)
            for half in range(2):
                wb = 6 + half
                for k in range(8):
                    MM(bank(wb), mergedT[pb_][:, k, tl * 128:(tl + 1) * 128], Wout[:, k, half * 512:(half + 1) * 512],
                       k == 0, k == 7, ['mergedT%d' % pb_, 'Wout'], [('ps', wb)])
                TT('vector', tmp3[bi][:, half * 512:(half + 1) * 512], bank(wb), gate1rep[:, half * 512:(half + 1) * 512],
                   ALU.mult, [('ps', wb), ('modrep', 2)], [('tmp3_%d' % bi, half)])
            TT('vector', x1t[bi], tmp3[bi], xo[bi], ALU.add, ['tmp3_%d' % bi, ('xo', bi)], [('x1t', bi)])
            DMA('sync', X1S[ti], x1t[bi], [('x1t', bi)], [('X1S', ti)], 'x1s%d' % bi)
            sm = small3[:, (ti % 4) * 4:(ti % 4) * 4 + 4]
            smk = ('small3', ti % 4)
            ACT(h2b[bi], x1t[bi], AF.Square, [('x1t', bi)], [('h2b', bi), smk], accum_out=sm[:, 0:1])
            ACT(sm[:, 1:2], sm[:, 0:1], AF.Sqrt, [smk], [smk], scale=1.0 / 1024.0, bias=epsb)
            RECIP(sm[:, 2:3], sm[:, 1:2], [smk], [smk])
            STT(tmp3[bi], x1t[bi], sm[:, 2:3], G2rep, ALU.mult, ALU.mult, [('x1t', bi), smk, 'G2rep'], ['tmp3_%d' % bi])
            TT('vector', h2b[bi], tmp3[bi], shift2rep, ALU.add, ['tmp3_%d' % bi, ('modrep', 3)], [('h2b', bi)])
            for k in range(8):
                TR(bank16(0)[:, k * 128:(k + 1) * 128], h2b[bi][:, k * 128:(k + 1) * 128], ident_bf,
                   [('h2b', bi), 'ident_bf'], [('ps', 0)])
            CP('scalar', h2T[:, :, ti * 128:(ti + 1) * 128], bank16(0).rearrange("p (k t) -> p k t", t=128),
               [('ps', 0)], [('h2T', ti)])

        for tl in range(chunks[0][1]):
            st_a(0, tl)
        for mt in range(8):
            st_b(0, mt)
        for cix in range(len(chunks)):
            ntl = chunks[cix][1]
            nxt = cix + 1 if cix + 1 < len(chunks) else None
            if nxt is not None:
                for tl in range(chunks[nxt][1]):
                    st_a(nxt, tl)
            mts = list(range(8)) if nxt is not None else []
            per = (len(mts) + ntl - 1) // ntl if mts else 0
            for tl in range(ntl):
                st_c(cix, tl)
                for mt in mts[tl * per:(tl + 1) * per]:
                    st_b(nxt, mt)
        P.emit()

        mT = Alloc(34, 122)([22, 2048], BF16)
        T4 = Alloc(122, 172)
        Wup = [T4([8, 256], BF16) for _ in range(2)]
        abuf = [T4([2050], F32) for _ in range(2)]
        ubuf = [T4([2048], F32) for _ in range(2)]
        sgu = T4([2048], BF16)
        convw = T4([3, 44], F32)
        convb = T4([44], F32)
        CDMA(convw, convw_d, ['convw'])
        CDMA(convb, convb_d, ['convb'])
        wup_v = wup_d.rearrange("(k p) c -> p k c", p=128)
        arr = [0]
        for fp in range(22):
            wi = fp % 2
            DMA('gpsimd', Wup[wi][:, :, 0:128], wup_v[:, :, fp * 128:(fp + 1) * 128], (), [('Wup', wi)], 'wu%d' % wi)
            DMA('gpsimd', Wup[wi][:, :, 128:256], wup_v[:, :, 2816 + fp * 128:2816 + (fp + 1) * 128], (), [('Wup', wi)], 'wu%d' % wi)
            for which in range(2):
                f = fp + 22 * which
                ab = abuf[which]
                ak = ('abuf', which)
                hps = psum[:, 7 * 512 + which * 2: 7 * 512 + which * 2 + 2]
                for k in range(8):
                    MM(hps, Wup[wi][:, k, which * 128:(which + 1) * 128], h2T[:, k, 126:128], k == 0, k == 7,
                       [('Wup', wi), 'h2T'], [('ps', 7)])
                ACT(ab[:, 0:2], hps, AF.Copy, [('ps', 7), 'hflag'], [ak], scale=hflag)
                for c in range(4):
                    bi = arr[0] % 6
                    arr[0] += 1
                    for k in range(8):
                        MM(bank(bi), Wup[wi][:, k, which * 128:(which + 1) * 128], h2T[:, k, 128 + c * 512: 128 + (c + 1) * 512],
                           k == 0, k == 7, [('Wup', wi), 'h2T'], [('ps', bi)])
                    CP('scalar', ab[:, 2 + c * 512: 2 + (c + 1) * 512], bank(bi), [('ps', bi)], [ak])
                ub = ubuf[which]
                uk = ('ubuf', which)
                ACT(ub, ab[:, 2:2050], AF.Identity, [ak, 'convw', 'convb'], [uk], scale=convw[:, 2, f:f + 1], bias=convb[:, f:f + 1])
                STT(ub, ab[:, 1:2049], convw[:, 1, f:f + 1], ub, ALU.mult, ALU.add, [ak, 'convw', uk], [uk])
                STT(ub, ab[:, 0:2048], convw[:, 0, f:f + 1], ub, ALU.mult, ALU.add, [ak, 'convw', uk], [uk])
            ACT(sgu, ubuf[0], AF.Silu, [('ubuf', 0)], ['sgu'])
            TT('vector', mT[:, fp, :], sgu, ubuf[1], ALU.mult, ['sgu', ('ubuf', 1)], [('mT', fp)])
        P.emit()

        T5 = Alloc(122, 206)
        Wd = T5([22, 1024], BF16)
        x1l = [T5([1024], F32) for _ in range(2)]
        ot = [T5([1024], F32) for _ in range(2)]
        tmp5 = T5([1024], F32)
        wd_v = wdown_d.rearrange("(k p) c -> p k c", p=128)
        WDQ = [0, 4, 10, 16, 22]
        for q in range(4):
            CDMA(Wd[:, WDQ[q]:WDQ[q + 1], :], wd_v[:, WDQ[q]:WDQ[q + 1], :], [('Wd', q)], cast=True)
        for ti in range(1, NOWN):
            s = ti % 2
            DMA('sync', x1l[s], X1S[ti], (), [('x1l', s)], 'x1l%d' % s)
            for half in range(2):
                yb = (ti % 2) * 2 + half
                for i in range(22):
                    MM(bank(yb), mT[:, i, (ti - 1) * 128: ti * 128], Wd[:, i, half * 512:(half + 1) * 512],
                       i == 0, i == 21, ['mT', ('Wd', 0 if i < 4 else (1 if i < 10 else (2 if i < 16 else 3)))], [('ps', yb)])
                TT('vector', tmp5[:, half * 512:(half + 1) * 512], bank(yb), gate2rep[:, half * 512:(half + 1) * 512],
                   ALU.mult, [('ps', yb), ('modrep', 5)], [('tmp5', half)])
            TT('vector', ot[s], tmp5, x1l[s], ALU.add, ['tmp5', ('x1l', s)], [('ot', s)])
            DMA('sync', out_d[(ti - 1) * 128: ti * 128, :], ot[s], [('ot', s)], (), 'out%d' % s)
        P.emit(final=True)
    return nc


def _t5_bucket_np(d):
    n = np.maximum(d, 0)
    nf = np.maximum(n, 1).astype(np.float32)
    large = 16 + (np.log(nf / np.float32(16)) / np.float32(np.log(128 / 16)) * np.float32(16)).astype(np.int32)
    large = np.minimum(large, 31)
    return np.where(n < 16, n, large)


_NC_CACHE = {}


def kernel(x, c, ada_w, ada_b, norm1_g, w_in, q_norm_g, k_norm_g, rel_bias, pool_w,
           pool_scale, w_branch_attn, w_branch_pool, w_out, norm2_g, w_up, conv_w, conv_b, w_down):
    f = lambda a: np.ascontiguousarray(np.asarray(a, dtype=np.float32))
    x = f(x); c = f(c)
    ada_w = f(ada_w)[0]; ada_b = f(ada_b)[0]; norm1_g = f(norm1_g)[0]; w_in = f(w_in)[0]
    q_norm_g = f(q_norm_g)[0]; k_norm_g = f(k_norm_g)[0]; rel_bias = f(rel_bias)
    pool_w = f(pool_w)[0]; pool_scale = f(pool_scale)[0]
    w_branch_attn = f(w_branch_attn)[0]; w_branch_pool = f(w_branch_pool)[0]; w_out = f(w_out)[0]
    norm2_g = f(norm2_g)[0]; w_up = f(w_up)[0]; conv_w = f(conv_w)[0]; conv_b = f(conv_b)[0]; w_down = f(w_down)[0]

    if 'nc' not in _NC_CACHE:
        _NC_CACHE['nc'] = build_nc()
    nc = _NC_CACHE['nc']

    kl = np.arange(128)[:, None]
    jj = np.arange(640)[None, :]
    dd = jj - 128 - kl
    bidx = _t5_bucket_np(dd)
    traw = np.zeros((128, 8, 640), np.float32)
    for h in range(8):
        traw[:, h, :] = np.where(dd >= 0, rel_bias[bidx, h], 0.0)
    cmask = np.where(dd >= 0, 0.0, NEG).astype(np.float32)
    rb31 = np.ascontiguousarray(np.broadcast_to(rel_bias[31][None, :], (128, 8))).astype(np.float32)
    onehot = np.zeros((32, 8192), np.float32)
    for n in range(32):
        onehot[n, n * 256:(n + 1) * 256] = 1.0
    onehot = onehot.astype(ml_dtypes.bfloat16)
    shiftm = np.zeros((128, 64), np.float32)
    shiftm[64 + np.arange(64), np.arange(64)] = 1.0
    wins = [2, 4, 8, 16]
    tp = np.arange(128)[:, None]
    tt = np.arange(128)[None, :]
    A_std = np.zeros((128, 16, 128), np.float32)
    for g, w in enumerate(wins):
        cur = np.where((tt - tp >= 0) & (tt - tp < w), 1.0 / w, 0.0) - (tp == tt)
        prev = np.where((tt + 128 - tp) < w, 1.0 / w, 0.0)
        A_std[:, g, :] = cur
        A_std[:, 4 + g, :] = prev
        A_std[:, 8 + g, :] = cur
        A_std[:, 12 + g, :] = prev
    A_first = A_std.copy()
    for g, w in enumerate(wins):
        cnt = np.minimum(tt + 1, w).astype(np.float32)
        cur = np.where((tt - tp >= 0) & (tt - tp < w), 1.0 / cnt, 0.0) - (tp == tt)
        A_first[:, 8 + g, :] = cur
        A_first[:, 12 + g, :] = 0.0
    rep = lambda v: np.ascontiguousarray(np.broadcast_to(v[None, :], (128, v.shape[0]))).astype(np.float32)
    common = {
        "ada_w": ada_w, "adab_rep": rep(ada_b), "g1rep": rep(norm1_g), "g2rep": rep(norm2_g),
        "w_in": w_in, "gkcol": np.ascontiguousarray(np.concatenate([k_norm_g, k_norm_g])[:, None]),
        "gqcol": np.ascontiguousarray(np.concatenate([q_norm_g, q_norm_g])[:, None]),
        "traw": traw, "cmask": cmask, "rb31": rb31, "onehot": onehot,
        "poolw": np.ascontiguousarray(pool_w.transpose(1, 0, 2)),
        "pscale": np.ascontiguousarray(pool_scale.reshape(4, 128).T),
        "wba": w_branch_attn, "wbp": w_branch_pool, "wout": w_out, "wup": w_up,
        "convw": np.ascontiguousarray(conv_w.reshape(3, 44, 128).transpose(2, 0, 1)),
        "convb": np.ascontiguousarray(conv_b.reshape(44, 128).T),
        "wdown": w_down, "shiftm": shiftm,
    }
    in_maps = []
    for core in range(8):
        b, j = core // 4, core % 4
        end = 2048 * (j + 1)
        start = end - 8192
        xwin = np.zeros((8192, 1024), np.float32)
        if start >= 0:
            xwin[:] = x[b, start:end]
        else:
            xwin[-start:] = x[b, 0:end]
        first_valid = 24 - 8 * j
        vb = np.zeros((128, 17, 32), np.float32)
        no = np.ones((128, 17, 32), np.float32)
        for ti in range(17):
            bb = (47 + ti) // 2
            row = np.full(32, -1e30, np.float32)
            row[first_valid:bb] = 0.0
            vb[:, ti, :] = row[None, :]
            no[:, ti, bb] = 0.0
        m = dict(common)
        m["xw"] = xwin
        m["cT"] = np.ascontiguousarray(c[b].reshape(8, 128).T)
        m["validb17"] = vb
        m["notown17"] = no.astype(ml_dtypes.bfloat16)
        m["amats"] = (A_first if j == 0 else A_std).astype(ml_dtypes.bfloat16)
        m["hflag"] = np.full((128, 1), 0.0 if j == 0 else 1.0, np.float32)
        in_maps.append(m)
    res = run_bass_kernel_spmd(nc, in_maps, core_ids=list(range(8)))
    out = np.zeros((2, 8192, 1024), np.float32)
    for core in range(8):
        b, j = core // 4, core % 4
        out[b, 2048 * j:2048 * (j + 1)] = res.results[core]["out"]
    return out
```

```python
import numpy as np
import ml_dtypes
from contextlib import ExitStack
import concourse.bass as bass
import concourse.mybir as mybir
from concourse.bass_utils import run_bass_kernel_spmd

F32 = mybir.dt.float32
BF16 = mybir.dt.bfloat16
AF = mybir.ActivationFunctionType
ALU = mybir.AluOpType
AX = mybir.AxisListType

ENGS = ['sync', 'scalar', 'vector', 'gpsimd', 'tensor']
NEG = -30000.0
EPS = 1e-6
NOWN = 17
SCHED_PHASES = [True, True, True, True, True, True]
QT = NOWN * 128


class Op:
    __slots__ = ('eng', 'fn', 'deps', 'flag', 'val', 'sem', 'is_dma', 'dur', 'fin', 'mode', 'force', 'slack', 'bg')


class Prog:
    def __init__(self, nc, stack):
        self.nc = nc
        self.stack = stack
        self.esem = {}
        self.ecount = {}
        for e in ['scalar', 'vector', 'gpsimd', 'tensor']:
            self.esem[e] = stack.enter_context(nc.semaphore('es_' + e))
            self.ecount[e] = 0
        self.dsem = {}
        self.state = {}
        self.ops = []
        self.waited = {e: {} for e in ENGS}
        self.phase_idx = 0
        self.cur_slack = 0.0
        self.extra = None

    def _conf(self, key):
        name, slot = key if isinstance(key, tuple) else (key, None)
        st = self.state.setdefault(name, {})
        if slot is None:
            return name, slot, list(st.keys())
        ks = []
        if slot in st:
            ks.append(slot)
        if None in st:
            ks.append(None)
        return name, slot, ks

    def op(self, eng, fn, reads=(), writes=(), dma=None, dur=0.3, mode=None):
        o = Op()
        o.dur = dur
        o.fin = None
        o.mode = mode
        o.force = None
        o.slack = self.cur_slack
        o.bg = False
        o.eng = eng
        o.fn = fn
        o.deps = []
        o.flag = False
        o.val = None
        o.sem = None
        o.is_dma = dma is not None
        deps = o.deps
        if self.extra is not None:
            deps.append(self.extra)
        for key in reads:
            name, slot, ks = self._conf(key)
            st = self.state[name]
            for k in ks:
                w = st[k][0]
                if w is not None:
                    deps.append(w)
        for key in writes:
            name, slot, ks = self._conf(key)
            st = self.state[name]
            for k in ks:
                w, rd = st[k]
                if w is not None:
                    deps.append(w)
                deps.extend(rd)
        for key in reads:
            name, slot = key if isinstance(key, tuple) else (key, None)
            st = self.state[name]
            if slot not in st:
                st[slot] = [None, []]
            st[slot][1].append(o)
        for key in writes:
            name, slot = key if isinstance(key, tuple) else (key, None)
            st = self.state[name]
            if slot is None:
                st.clear()
            st[slot] = [o, []]
        if dma is not None:
            if dma not in self.dsem:
                self.dsem[dma] = [self.stack.enter_context(self.nc.semaphore('ds_' + dma)), 0, None]
            d = self.dsem[dma]
            if d[2] is not None:
                deps.append(d[2])
            d[1] += 16
            d[2] = o
            o.sem = d[0]
            o.val = d[1]
        self.ops.append(o)
        self.last = o
        return o

    def schedule(self, W=48, LAT=0.35):
        pend = {e: [] for e in ENGS}
        for o in self.ops:
            pend[o.eng].append(o)
        ptr = {e: 0 for e in ENGS}
        win = {e: [] for e in ENGS}
        tnow = {e: 0.0 for e in ENGS}
        pemode = [None]
        out = []
        remaining = len(self.ops)
        while remaining:
            best = None
            for e in ENGS:
                w = win[e]
                lst = pend[e]
                while len(w) < W and ptr[e] < len(lst):
                    w.append(lst[ptr[e]])
                    ptr[e] += 1
                te = tnow[e]
                for o in w:
                    ready = te
                    ok = True
                    sl = o.slack
                    for d in o.deps:
                        f = d.fin
                        if f is None:
                            ok = False
                            break
                        f += (LAT if (d.eng != e or d.is_dma) else 0.15) + sl
                        if f > ready:
                            ready = f
                    if ok and e == 'tensor' and o.mode != pemode[0]:
                        ready = max(ready, te + 0.4)
                    if ok and o.bg:
                        ready = max(ready, te) + 0.25
                    if ok and (best is None or ready < best[0] - 1e-9):
                        best = (ready, e, o)
                    if ok and ready <= te:
                        break
            ready, e, o = best
            win[e].remove(o)
            if e == 'tensor':
                pemode[0] = o.mode
            if o.is_dma:
                o.fin = ready + o.dur
                tnow[e] = ready + 0.06
            else:
                o.fin = ready + o.dur
                tnow[e] = o.fin
            out.append(o)
            remaining -= 1
        self.ops = out
        self.sim_time = max(tnow.values())

    def emit(self, final=False):
        nc = self.nc
        if SCHED_PHASES[self.phase_idx]:
            self.schedule()
        self.phase_idx += 1
        prev = None
        for o in self.ops:
            if o.eng == 'tensor' and not o.is_dma:
                if prev is not None and prev.mode != o.mode:
                    o.force = prev
                    prev.flag = True
                prev = o
        for o in self.ops:
            for d in o.deps:
                if not d.is_dma:
                    if d.eng == 'tensor' and o.eng == 'tensor' and not o.is_dma:
                        continue
                    d.flag = True
        for o in self.ops:
            if not o.is_dma and o.flag:
                self.ecount[o.eng] += 1
                o.sem = self.esem[o.eng]
                o.val = self.ecount[o.eng]
        per = {e: [] for e in ENGS}
        for o in self.ops:
            per[o.eng].append(o)
        dma_final = [(d[0], d[1]) for d in self.dsem.values() if d[1] > 0]

        def replay(eng_name):
            def body(e):
                wt = self.waited[eng_name]
                for o in per[eng_name]:
                    need = {}
                    for d in o.deps:
                        if d.val is None:
                            continue
                        if (not d.is_dma) and d.eng == 'tensor' and eng_name == 'tensor' and not o.is_dma:
                            continue
                        k = id(d.sem)
                        if k not in need or need[k][1] < d.val:
                            need[k] = (d.sem, d.val)
                    if o.force is not None:
                        d = o.force
                        k = id(d.sem)
                        if k not in need or need[k][1] < d.val:
                            need[k] = (d.sem, d.val)
                    for k, (sem, val) in need.items():
                        if wt.get(k, 0) >= val:
                            continue
                        e.wait_ge(sem, val)
                        wt[k] = val
                    ins = o.fn(e)
                    if o.is_dma:
                        ins.then_inc(o.sem, 16)
                    elif o.flag:
                        ins.then_inc(o.sem, 1)
                if eng_name in ('sync', 'gpsimd') or final:
                    for sem, val in dma_final:
                        k = id(sem)
                        if wt.get(k, 0) >= val:
                            continue
                        e.wait_ge(sem, val)
                        wt[k] = val
            return body

        with nc.Block() as block:
            block.sync(replay('sync'))
            block.scalar(replay('scalar'))
            block.vector(replay('vector'))
            block.gpsimd(replay('gpsimd'))
            block.tensor(replay('tensor'))
        self.ops = []
        self.state = {}
        for d in self.dsem.values():
            d[2] = None


def build_nc():
    nc = bass.Bass("TRN2", target_bir_lowering=False)
    din = lambda n, s, d=F32: nc.dram_tensor(n, list(s), d, kind="ExternalInput").ap()
    xw = din("xw", [8192, 1024])
    cT_d = din("cT", [128, 8])
    adaw_d = din("ada_w", [1024, 6144])
    adab_d = din("adab_rep", [128, 6144])
    g1rep_d = din("g1rep", [128, 1024])
    g2rep_d = din("g2rep", [128, 1024])
    win_d = din("w_in", [1024, 4096])
    gkcol_d = din("gkcol", [128, 1])
    gqcol_d = din("gqcol", [128, 1])
    traw_d = din("traw", [128, 8, 640])
    cmask_d = din("cmask", [128, 640])
    rb31_d = din("rb31", [128, 8])
    onehot_d = din("onehot", [32, 8192], BF16)
    validb_d = din("validb17", [128, 17, 32])
    notown_d = din("notown17", [128, 17, 32], BF16)
    amat_d = din("amats", [128, 16, 128], BF16)
    hflag_d = din("hflag", [128, 1])
    poolw_d = din("poolw", [128, 4, 128])
    pscale_d = din("pscale", [128, 4])
    wba_d = din("wba", [512, 1024])
    wbp_d = din("wbp", [512, 1024])
    wout_d = din("wout", [1024, 1024])
    wup_d = din("wup", [1024, 5632])
    convw_d = din("convw", [128, 3, 44])
    convb_d = din("convb", [128, 44])
    wdown_d = din("wdown", [2816, 1024])
    shiftm_d = din("shiftm", [128, 64])
    out_d = nc.dram_tensor("out", [2048, 1024], F32, kind="ExternalOutput").ap()
    KS = nc.dram_tensor("KS", [8, 64, 8192], BF16, kind="Internal").ap()
    VS = nc.dram_tensor("VS", [8, 128, 64 * 64], BF16, kind="Internal").ap()
    QS = nc.dram_tensor("QS", [8, 64, QT], BF16, kind="Internal").ap()
    SGS = nc.dram_tensor("SGS", [16, 128, QT], BF16, kind="Internal").ap()
    X1S = nc.dram_tensor("X1S", [NOWN, 128, 1024], F32, kind="Internal").ap()
    THS = nc.dram_tensor("THS", [128, 8, 640], BF16, kind="Internal").ap()
    TLS = nc.dram_tensor("TLS", [128, 8, 640], BF16, kind="Internal").ap()

    ARENA_KB = 206
    with ExitStack() as st:
        arena = st.enter_context(nc.sbuf_tensor("arena", [128, ARENA_KB * 256], F32))
        psum = st.enter_context(nc.psum_tensor("psum", [128, 4096], F32))
        a32 = arena
        a16 = arena.bitcast(BF16)
        ps16 = psum.bitcast(BF16)
        P = Prog(nc, st)

        class Alloc:
            def __init__(self, base_kb, limit_kb):
                self.off = int(base_kb * 1024)
                self.limit = int(limit_kb * 1024)

            def __call__(self, shape, dt, parts=128):
                n = 1
                for s in shape:
                    n *= s
                esz = 4 if dt == F32 else 2
                nb = (n * esz + 63) // 64 * 64
                off = self.off
                self.off += nb
                assert self.off <= self.limit, (self.off, self.limit)
                base = a32 if dt == F32 else a16
                v = base[0:parts, off // esz: off // esz + n]
                if len(shape) == 2:
                    v = v.rearrange("p (a b) -> p a b", b=shape[1])
                elif len(shape) == 3:
                    v = v.rearrange("p (a b c) -> p a b c", b=shape[1], c=shape[2])
                return v

        def bank(i, n=512, parts=128, off=0):
            return psum[0:parts, i * 512 + off: i * 512 + off + n]

        def bank16(i, n=1024, parts=128):
            return ps16[0:parts, i * 1024: i * 1024 + n]

        def fs(ap):
            n = 1
            for d in ap.shape[1:]:
                n *= d
            return n

        def rk(n):
            return 32 if n <= 32 else (64 if n <= 64 else 128)

        def edur(eng, n):
            if eng == 'scalar':
                return 0.15 + n / 1150.0
            if eng == 'vector':
                return 0.16 + n / 950.0
            return 0.25 + n / 450.0

        def MM(out, lhsT, rhs, start, stop, R, W):
            d = max(0.07, 0.01 + fs(rhs) / 2350.0)
            if lhsT.dtype == F32:
                d *= 4
            P.op('tensor', lambda e: e.matmul(out, lhsT=lhsT, rhs=rhs, start=start, stop=stop), R, W, dur=d,
                 mode=(rk(lhsT.shape[0]), rk(fs(lhsT)), lhsT.dtype == F32, False))

        def TR(out, in_, ident, R, W):
            P.op('tensor', lambda e: e.transpose(out, in_, ident), R, W, dur=(0.42 if in_.dtype == F32 else 0.12),
                 mode=(rk(in_.shape[0]), rk(fs(in_)), in_.dtype == F32, True))

        def ACT(out, in_, func, R, W, **kw):
            P.op('scalar', lambda e: e.activation(out=out, in_=in_, func=func, **kw), R, W,
                 dur=edur('scalar', fs(in_)) + (0.1 if 'accum_out' in kw else 0.0))

        def DMA(q, out, in_, R, W, sem):
            nb = fs(out) * out.shape[0] * (4 if out.dtype == F32 else 2)
            P.op(q, lambda e: e.dma_start(out=out, in_=in_), R, W, dma=sem, dur=2.0 + nb / 120000.0)

        def TT(eng, out, in0, in1, op, R, W):
            P.op(eng, lambda e: e.tensor_tensor(out=out, in0=in0, in1=in1, op=op), R, W, dur=edur(eng, fs(out)))

        def TS(eng, out, in0, s1, s2, op0, op1, R, W):
            if op1 is None:
                P.op(eng, lambda e: e.tensor_scalar(out=out, in0=in0, scalar1=s1, scalar2=None, op0=op0), R, W,
                     dur=edur(eng, fs(out)))
            else:
                P.op(eng, lambda e: e.tensor_scalar(out=out, in0=in0, scalar1=s1, scalar2=s2, op0=op0, op1=op1), R, W,
                     dur=edur(eng, fs(out)))

        def STT(out, in0, scalar, in1, op0, op1, R, W):
            P.op('vector', lambda e: e.scalar_tensor_tensor(out=out, in0=in0, scalar=scalar, in1=in1, op0=op0, op1=op1), R, W,
                 dur=edur('vector', fs(out)))

        def CP(eng, out, in_, R, W):
            if eng == 'scalar':
                P.op('scalar', lambda e: e.copy(out, in_), R, W, dur=edur(eng, fs(out)))
            else:
                P.op(eng, lambda e: e.tensor_copy(out, in_), R, W, dur=edur(eng, fs(out)))

        def MEMSET(eng, ap, val, W):
            P.op(eng, lambda e: e.memset(ap, val), (), W, dur=edur(eng, fs(ap)))

        def RECIP(out, in_, R, W):
            P.op('vector', lambda e: e.reciprocal(out, in_), R, W, dur=0.08 + 8 * fs(out) / 900.0)

        crr = [0]

        def CDMA(out, in_, W, cast=False):
            crr[0] += 1
            q = 'gpsimd' if cast else 'sync'
            DMA(q, out, in_, (), W, ('cg%d' if cast else 'cs%d') % (crr[0] % 4))

        G = Alloc(0, 34)
        modrep = G([6144], F32)
        G1rep = G([1024], F32)
        G2rep = G([1024], F32)
        ident_bf = G([128], BF16)
        ident_f = G([128], F32)
        gkcol = G([1], F32)
        gqcol = G([1], F32)
        gk256 = G([1], F32)
        hflag = G([1], F32)
        onesf = G([2], F32)
        shift1rep = modrep[:, 0:1024]
        gate1rep = modrep[:, 2048:3072]
        shift2rep = modrep[:, 3072:4096]
        gate2rep = modrep[:, 5120:6144]

        p_own = Alloc(34, 51)([NOWN, 512], BF16)
        attnT = Alloc(51, 85)([8, QT], BF16, parts=64)
        h2T = Alloc(172, 206)([8, QT], BF16)
        kmeanT = G([8, 32], BF16)

        T0 = Alloc(51, 200)
        cT = T0([8], F32)
        cact = T0([8], F32)
        cbc = T0([8, 128], F32)
        g1r = T0([1024], F32)
        g2r = T0([1024], F32)
        adab = T0([6144], F32)
        adaw = [T0([3072], F32) for _ in range(3)]
        identtmp = T0([128], F32)

        CDMA(cT, cT_d, ['cT'])
        CDMA(adab, adab_d, ['adab'])
        CDMA(g1r, g1rep_d, ['g1r'])
        CDMA(g2r, g2rep_d, ['g2r'])
        CDMA(gkcol, gkcol_d, ['gkcol'])
        CDMA(gqcol, gqcol_d, ['gqcol'])
        CDMA(hflag, hflag_d, ['hflag'])
        MEMSET('gpsimd', ident_f, 1.0, ['ident_f'])
        P.op('gpsimd', lambda e: e.affine_select(out=ident_f, in_=ident_f, pattern=[[-1, 128]],
                                                 compare_op=ALU.is_equal, fill=0.0, base=0, channel_multiplier=1),
             ['ident_f'], ['ident_f'])
        CP('vector', ident_bf, ident_f, ['ident_f'], ['ident_bf'])
        MEMSET('vector', onesf, 1.0, ['onesf'])
        onesb = G([2], BF16)
        MEMSET('vector', onesb, 1.0, ['onesb'])
        epsb = G([1], F32)
        MEMSET('vector', epsb, EPS, ['epsb'])
        ACT(cact, cT, AF.Silu, ['cT'], ['cact'])
        CP('vector', cbc, cact.unsqueeze(2).to_broadcast([128, 8, 128]), ['cact'], ['cbc'])
        P.op('scalar', lambda e: e.mul(gk256, gkcol, 1.0 / 256.0), ['gkcol'], ['gk256'])
        P.op('scalar', lambda e: e.mul(gqcol, gqcol, 0.125), ['gqcol'], ['gqcol'])
        adaw_v = adaw_d.rearrange("(k p) c -> k p c", p=128)
        cnt = 0
        for hf in range(2):
            for kt in range(8):
                s = cnt % 3
                cnt += 1
                DMA('sync', adaw[s], adaw_v[kt, :, hf * 3072:(hf + 1) * 3072], (), [('adaw', s)], 'adaw%d' % s)
                for c in range(6):
                    MM(bank(c), cbc[:, kt, :], adaw[s][:, c * 512:(c + 1) * 512], kt == 0, kt == 7,
                       [('adaw', s), 'cbc'], [('ps', c)])
            for c in range(6):
                col = hf * 3072 + c * 512
                TT('vector', modrep[:, col:col + 512], bank(c), adab[:, col:col + 512], ALU.add,
                   [('ps', c), 'adab'], [('modrep', col // 1024)])
        Thi0 = T0([8, 640], BF16)
        Tlo0 = T0([8, 640], BF16)
        tA0 = T0([640], F32)
        tB0 = T0([640], F32)
        cmask0 = T0([640], F32)
        rb310 = T0([8], F32)
        CDMA(cmask0, cmask_d, ['cmask0'])
        CDMA(rb310, rb31_d, ['rb310'])
        for h in range(8):
            DMA('sync', tA0, traw_d[:, h, :], (), ['tA0'], 'tra')
            STT(tA0, tA0, rb310[:, h:h + 1], cmask0, ALU.subtract, ALU.add, ['tA0', 'rb310', 'cmask0'], ['tA0'])
            CP('vector', Thi0[:, h, :], tA0, ['tA0'], [('Thi0', h)])
            TT('vector', tB0, tA0, Thi0[:, h, :], ALU.subtract, ['tA0', ('Thi0', h)], ['tB0'])
            CP('vector', Tlo0[:, h, :], tB0, ['tB0'], [('Tlo0', h)])
        DMA('sync', THS, Thi0, ['Thi0'], ['THS'], 'ths')
        DMA('sync', TLS, Tlo0, ['Tlo0'], ['TLS'], 'ths')
        STT(G1rep, modrep[:, 1024:2048], 1.0, g1r, ALU.add, ALU.mult, [('modrep', 1), 'g1r'], ['G1rep'])
        STT(G2rep, modrep[:, 4096:5120], 1.0, g2r, ALU.add, ALU.mult, [('modrep', 4), 'g2r'], ['G2rep'])
        P.emit()

        T1 = Alloc(51, 206)
        Wkv = T1([8, 1024], BF16)
        Wqp = T1([8, 1024], BF16)
        Wg = T1([8, 2048], BF16)
        xt = [T1([1024], F32) for _ in range(2)]
        tmp = T1([1024], F32)
        hb = [T1([1024], BF16) for _ in range(2)]
        hTs = [T1([8, 128], BF16) for _ in range(2)]
        hTg = [T1([8, 512], BF16) for _ in range(2)]
        sq = T1([512], F32)
        knp = [T1([512], BF16) for _ in range(2)]
        qnp = [T1([512], BF16) for _ in range(2)]
        Kst = [T1([4, 512], BF16) for _ in range(2)]
        Vst = [T1([8, 4 * 64], BF16) for _ in range(2)]
        Qst = T1([4, 512], BF16)
        sgst = T1([8, 512], BF16)
        junk = T1([1024], BF16)
        small = T1([96], F32)

        win_v = win_d.rearrange("(k p) c -> p k c", p=128)
        CDMA(Wkv, win_v[:, :, 512:1536], ['Wkv'], cast=True)

        def late_w_loads():
            P.extra = P.last
            CDMA(Wqp[:, :, 0:512], win_v[:, :, 0:512], [('Wqp', 0)], cast=True)
            CDMA(Wqp[:, :, 512:1024], win_v[:, :, 1536:2048], [('Wqp', 1)], cast=True)
            CDMA(Wg[:, :, 0:1024], win_v[:, :, 2048:3072], [('Wg', 0)], cast=True)
            CDMA(Wg[:, :, 1024:2048], win_v[:, :, 3072:4096], [('Wg', 1)], cast=True)
            P.extra = None

        xw_v = xw.rearrange("(t p) c -> t p c", p=128)
        KS_v = KS.rearrange("(hp two) d t -> (two d) hp t", two=2)
        QS_v = QS.rearrange("(hp two) d t -> (two d) hp t", two=2)
        VS_v = VS.rearrange("h p x -> p h x")
        SGS_v = SGS.rearrange("g p t -> p g t")
        zrr = [0]

        def zbank():
            zrr[0] += 1
            return 1 + (zrr[0] % 3)

        def hT_of(wt):
            ti = wt - 47
            if ti <= 0:
                return hTs[wt % 2], ('hTs', wt % 2)
            tl = (ti - 1) % 4
            gp = ((ti - 1) // 4) % 2
            return hTg[gp][:, :, tl * 128:(tl + 1) * 128], ('hTg%d' % gp, tl)

        def stageA(wt):
            s = wt % 2
            sm = small[:, (wt % 8) * 4:(wt % 8) * 4 + 4]
            smk = ('small', wt % 8)
            DMA('sync', xt[s], xw_v[wt], (), [('xt', s)], 'xt%d' % s)
            ACT(junk, xt[s], AF.Square, [('xt', s)], ['junk', smk], accum_out=sm[:, 0:1])
            ACT(sm[:, 1:2], sm[:, 0:1], AF.Sqrt, [smk], [smk], scale=1.0 / 1024.0, bias=epsb)
            RECIP(sm[:, 2:3], sm[:, 1:2], [smk], [smk])
            STT(tmp, xt[s], sm[:, 2:3], G1rep, ALU.mult, ALU.mult, [('xt', s), smk, 'G1rep'], ['tmp'])
            TT('vector', hb[s], tmp, shift1rep, ALU.add, ['tmp', ('modrep', 0)], [('hb', s)])
            tpb = 0 if s == 0 else 7
            for k in range(8):
                TR(bank16(tpb)[:, k * 128:(k + 1) * 128], hb[s][:, k * 128:(k + 1) * 128], ident_bf,
                   [('hb', s), 'ident_bf'], [('ps', tpb)])
            hv, hk = hT_of(wt)
            CP('scalar', hv, bank16(tpb).rearrange("p (k t) -> p k t", t=128), [('ps', tpb)], [hk])

        def headnorm(zb, dst, dkey, smk, sm):
            ACT(sq, bank(zb), AF.Square, [('ps', zb)], ['sq'])
            P.op('vector', lambda e: e.tensor_reduce(out=sm, in_=sq.rearrange("p (h d) -> p h d", d=64),
                                                      axis=AX.X, op=ALU.add), ['sq'], [smk])
            ACT(sm, sm, AF.Sqrt, [smk], [smk], scale=1.0 / 64.0, bias=epsb)
            RECIP(sm, sm, [smk], [smk])
            TT('vector', dst.rearrange("p (h d) -> p h d", d=64), bank(zb).rearrange("p (h d) -> p h d", d=64),
               sm.unsqueeze(2).to_broadcast([128, 8, 64]), ALU.mult, [('ps', zb), smk], [dkey])

        def stageB(wt):
            ti = wt - 47
            hv, hk = hT_of(wt)
            s = wt % 2
            zk = zbank()
            for k in range(8):
                MM(bank(zk), hv[:, k, :], Wkv[:, k, 0:512], k == 0, k == 7, [hk, 'Wkv'], [('ps', zk)])
            zv = zbank()
            for k in range(8):
                MM(bank(zv), hv[:, k, :], Wkv[:, k, 512:1024], k == 0, k == 7, [hk, 'Wkv'], [('ps', zv)])
            smk = ('smallk', wt % 4)
            sm = small[:, 48 + (wt % 4) * 8: 48 + (wt % 4) * 8 + 8]
            headnorm(zk, knp[s], ('knp', s), smk, sm)
            gv, tlv = (wt // 4) % 2, wt % 4
            CP('scalar', Vst[gv].rearrange("p h (t d) -> p h t d", d=64)[:, :, tlv, :],
               bank(zv).rearrange("p (h d) -> p h d", d=64), [('ps', zv)], [('Vst', gv)])
            if tlv == 3:
                t0 = wt - 3
                DMA('sync', VS_v[:, :, t0 * 64:(t0 + 4) * 64], Vst[gv], [('Vst', gv)], [('VS', wt // 4)], 'vst%d' % gv)
            if ti >= 0:
                zq = zbank()
                for k in range(8):
                    MM(bank(zq), hv[:, k, :], Wqp[:, k, 0:512], k == 0, k == 7, [hk, ('Wqp', 0)], [('ps', zq)])
                zp = zbank()
                for k in range(8):
                    MM(bank(zp), hv[:, k, :], Wqp[:, k, 512:1024], k == 0, k == 7, [hk, ('Wqp', 1)], [('ps', zp)])
                smq = ('smallq', wt % 2)
                smqv = small[:, 32 + (wt % 2) * 8: 32 + (wt % 2) * 8 + 8]
                headnorm(zq, qnp[s], ('qnp', s), smq, smqv)
                CP('scalar', p_own[:, ti, :], bank(zp), [('ps', zp)], [('p_own', ti)])

        def stageC(wt):
            ti = wt - 47
            s = wt % 2
            for hp in range(4):
                TR(bank16(4)[:, hp * 128:(hp + 1) * 128], knp[s][:, hp * 128:(hp + 1) * 128], ident_bf,
                   [('knp', s), 'ident_bf'], [('ps', 4)])
            for hp in range(4):
                MM(psum[:, 6 * 512 + hp * 64 + wt: 6 * 512 + hp * 64 + wt + 1], knp[s][:, hp * 128:(hp + 1) * 128],
                   onesb[:, 0:1], True, True, [('knp', s), 'onesb'], [('ps', 6)])
            gk_, tlk = (wt // 4) % 2, wt % 4
            ACT(Kst[gk_][:, :, tlk * 128:(tlk + 1) * 128],
                bank16(4, 512).rearrange("p (h t) -> p h t", t=128), AF.Copy,
                [('ps', 4), 'gkcol'], [('Kst', gk_)], scale=gkcol)
            if tlk == 3:
                t0 = wt - 3
                DMA('sync', KS_v[:, :, t0 * 128:(t0 + 4) * 128], Kst[gk_], [('Kst', gk_)], [('KS', wt // 4)], 'kst%d' % gk_)
            if ti >= 0:
                for hp in range(4):
                    TR(bank16(5)[:, hp * 128:(hp + 1) * 128], qnp[s][:, hp * 128:(hp + 1) * 128], ident_bf,
                       [('qnp', s), 'ident_bf'], [('ps', 5)])
                tl = 0 if ti == 0 else (ti - 1) % 4
                ACT(Qst[:, :, tl * 128:(tl + 1) * 128],
                    bank16(5, 512).rearrange("p (h t) -> p h t", t=128), AF.Copy,
                    [('ps', 5), 'gqcol'], ['Qst'], scale=gqcol)
                if ti == 0 or tl == 3:
                    n = 128 if ti == 0 else 512
                    tok0 = 0 if ti == 0 else (ti - 3) * 128
                    DMA('sync', QS_v[:, :, tok0:tok0 + n], Qst[:, :, 0:n], ['Qst'], [('QS', ti)], 'qst')
                    if ti == 0:
                        hgv, hgk = hTs[wt % 2], [('hTs', wt % 2)]
                    else:
                        gp = ((ti - 1) // 4) % 2
                        hgv, hgk = hTg[gp], ['hTg%d' % gp]
                    for gt in range(16):
                        zg = zbank()
                        for k in range(8):
                            MM(bank(zg, n), Wg[:, k, gt * 128:(gt + 1) * 128], hgv[:, k, 0:n], k == 0, k == 7,
                               hgk + [('Wg', gt // 8)], [('ps', zg)])
                        ACT(sgst[:, gt % 8, 0:n], bank(zg, n), AF.Sigmoid, [('ps', zg)], [('sgst', gt % 8)])
                        if gt % 8 == 7:
                            g0 = gt - 7
                            DMA('sync', SGS_v[:, g0:g0 + 8, tok0:tok0 + n], sgst[:, :, 0:n], ['sgst'], [('SGS', ti)], 'sgst')

        stageA(0)
        for wt in range(64):
            if wt == 6:
                late_w_loads()
            if wt + 1 < 64:
                stageA(wt + 1)
            stageB(wt)
            if wt >= 1:
                stageC(wt - 1)
        stageC(63)
        kms = T1([4, 32], F32)
        kmp = T1([4, 32], BF16)
        ksv = psum[:, 6 * 512: 6 * 512 + 256].rearrange("p (h b two) -> p h b two", h=4, two=2)
        CP('scalar', kms, ksv[:, :, :, 0], [('ps', 6)], ['kms'])
        TT('vector', kms, kms, ksv[:, :, :, 1], ALU.add, ['kms', ('ps', 6)], ['kms'])
        ACT(kmp, kms, AF.Copy, ['kms', 'gk256'], ['kmp'], scale=gk256)
        kmT4 = kmeanT[0:64].rearrange("p (hp two) b -> p hp two b", two=2)
        DMA('sync', kmT4[:, :, 0, :], kmp[0:64], ['kmp'], [('kmeanT', 0)], 'kmt')
        DMA('sync', kmT4[:, :, 1, :], kmp[64:128], ['kmp'], [('kmeanT', 1)], 'kmt')
        P.emit()

        T2 = Alloc(85, 206)
        Kaug = [T2([8192], BF16, parts=96) for _ in range(2)]
        Vaug = [T2([64, 128], BF16) for _ in range(2)]
        Vld = T2([64 * 64], BF16)
        Qaug = [T2([QT], BF16, parts=96) for _ in range(2)]
        Thi = T2([8, 640], BF16)
        Tlo = T2([8, 640], BF16)
        cmask = T2([640], F32)
        rb31 = T2([8], F32)
        validb = T2([17, 32], F32)
        notown = T2([17, 32], BF16)
        shiftm = T2([64], F32)
        gm = T2([NOWN], F32)
        sc_off = T2.off
        PT = [T2([4, 256], BF16) for _ in range(3)]
        tmpA = T2([640], F32)
        tmpB = T2([640], F32)
        sc_end = T2.off
        GA_ = Alloc(sc_off / 1024.0, sc_end / 1024.0)
        gA = GA_([NOWN, 32], F32)
        gB = GA_([NOWN, 32], F32)
        gE = GA_([NOWN, 32], F32)
        Mf = GA_([NOWN, 96], BF16)
        SCK = [('PT', 0), ('PT', 1), ('PT', 2), 'tmpA', 'tmpB']
        Osb = [T2([256], F32) for _ in range(2)]

        CDMA(validb, validb_d, ['validb'])
        CDMA(notown, notown_d, ['notown'])
        CDMA(shiftm, shiftm_d, ['shiftm'])
        CDMA(Kaug[0][64:96], onehot_d, [('Kaug1h', 0)])
        for i in range(2):
            MEMSET('gpsimd', Vaug[i][:, :, 64:128], 1.0, [('Vaug1', i)])

        srr = [0]
        prr = [0]
        orr = [0]
        mrr = [0]
        def p2_loads(h):
            hbuf = h % 2
            DMA('sync', Kaug[hbuf][0:64], KS[h], ['KS'], [('Kaug', hbuf)], 'ka%d' % hbuf)
            DMA('sync', Vld, VS[h], ['VS'], ['Vld'], 'vld')
            vl3 = Vld.rearrange("p (t d) -> p t d", d=64)
            for qd in range(4):
                CP('vector', Vaug[hbuf][:, qd * 16:(qd + 1) * 16, 0:64], vl3[:, qd * 16:(qd + 1) * 16, :],
                   ['Vld'], [('Vaug', hbuf)])
            DMA('sync', Qaug[hbuf][0:64], QS[h], ['QS'], [('Qaug', hbuf)], 'qa%d' % hbuf)

        def p2_gate_all(h):
            hbuf = h % 2
            nt = NOWN
            gps = psum[:, 0:nt * 32].rearrange("p (t n) -> p t n", n=32)
            gk = [('ps', 0), ('ps', 1)]
            for ti in range(nt):
                MM(gps[:, ti, :], Qaug[hbuf][0:64, ti * 128:(ti + 1) * 128], kmeanT[0:64, h, :], True, True,
                   [('Qaug', hbuf), 'kmeanT'], [('ps', ti // 16)])
            m = gm
            mb = m.unsqueeze(2).to_broadcast([128, nt, 32])

            def rmax(src):
                P.op('vector', lambda e: e.tensor_reduce(out=m, in_=src, axis=AX.X, op=ALU.max), SCK, ['gm'],
                     dur=0.15 + nt * 32 / 900.0)
            TT('vector', gA, gps, validb, ALU.add, gk + ['validb'], SCK)
            rmax(gA)
            TT('vector', gE, gA, mb, ALU.is_equal, ['gm'], SCK)
            STT(gB, gE, -2e30, gA, ALU.mult, ALU.add, [], SCK)
            rmax(gB)
            TT('vector', gE, gB, mb, ALU.is_equal, ['gm'], SCK)
            STT(gB, gE, -2e30, gB, ALU.mult, ALU.add, [], SCK)
            rmax(gB)
            TS('vector', m, m, -1e29, None, ALU.max, None, ['gm'], ['gm'])
            TT('vector', gE, gA, mb, ALU.is_ge, ['gm'], SCK)
            TS('vector', gE, gE, 1.0, -NEG, ALU.subtract, ALU.mult, [], SCK)
            TT('vector', Mf[:, :, 64:96], gE, notown, ALU.mult, ['notown'], SCK)
            for ti in range(nt):
                bk = 2 + ti // 8
                TR(ps16[0:96, bk * 1024 + (ti % 8) * 128: bk * 1024 + (ti % 8 + 1) * 128], Mf[:, ti, :], ident_bf,
                   SCK + ['ident_bf'], [('ps', bk)])
            for bk in range(2, 5):
                t_lo, t_hi = (bk - 2) * 8, min(nt, (bk - 1) * 8)
                if t_lo >= nt:
                    continue
                n_ = (t_hi - t_lo) * 128
                CP('vector', Qaug[hbuf][64:96, t_lo * 128: t_hi * 128], ps16[64:96, bk * 1024: bk * 1024 + n_],
                   [('ps', bk)], [('Qmask', hbuf)])

        GBATCH = [(0, 1), (1, 4), (5, 4), (9, 4), (13, 4)]

        pending_fin = []

        def p2_chunk(h, ci, hooks):
            hbuf = h % 2
            if ci == 0:
                b, nq, q0, qc0 = 23, 128, 128, 0
            else:
                b, nq, q0, qc0 = 23 + ci, 256, 0, (2 * ci - 1) * 128
            oi = orr[0] % 2
            orr[0] += 1
            ob = 6
            Ops = bank(ob, nq)
            Kk = [('Kaug', hbuf), ('Kaug1h', hbuf)]
            far = list(range(0, b - 1))
            fgroups = [far[i:i + 2] for i in range(0, len(far), 2)]
            groups = [(fgroups[0], False), ([b - 1, b], True)] + [(g, False) for g in fgroups[1:]]
            npv = sum(2 * len(g) for g, _ in groups)
            pvc = 0
            for gi, (grp, near) in enumerate(groups):
                sp = srr[0] % 3
                srr[0] += 1
                sv = psum[:, sp * 1024:(sp + 1) * 1024].rearrange("p (k q) -> p k q", q=256)
                skeys = [('ps', 2 * sp + i) for i in range(len(grp))]
                for i, n in enumerate(grp):
                    for ktl in range(2):
                        kt_ = 2 * n + ktl
                        MM(sv[:, 2 * i + ktl, 0:nq], Kaug[hbuf][0:96, kt_ * 128:(kt_ + 1) * 128],
                           Qaug[hbuf][0:96, qc0:qc0 + nq], True, not near,
                           Kk + [('Qaug', hbuf), ('Qmask', hbuf)], [('ps', 2 * sp + i)])
                        if near:
                            j0 = q0 + (128 if n == b else 384) - 128 * ktl
                            MM(sv[:, 2 * i + ktl, 0:nq], ident_bf, Thi[:, h, j0:j0 + nq], False, False,
                               ['ident_bf', ('Thi', h)], [('ps', 2 * sp + i)])
                            MM(sv[:, 2 * i + ktl, 0:nq], ident_bf, Tlo[:, h, j0:j0 + nq], False, True,
                               ['ident_bf', ('Tlo', h)], [('ps', 2 * sp + i)])
                pi = prr[0] % 3
                prr[0] += 1
                nk = 2 * len(grp)
                ACT(PT[pi][:, 0:nk, 0:nq], sv[:, 0:nk, 0:nq], AF.Exp, skeys, [('PT', pi)])
                for i, n in enumerate(grp):
                    for ktl in range(2):
                        kt_ = 2 * n + ktl
                        MM(Ops, Vaug[hbuf][:, kt_, :], PT[pi][:, 2 * i + ktl, 0:nq], pvc == 0,
                           pvc == npv - 1, [('Vaug', hbuf), ('Vaug1', hbuf), ('PT', pi)], [('ps', ob)])
                        pvc += 1
                if gi in hooks:
                    for fnc in hooks[gi]:
                        fnc(P.last)
            CP('vector', Osb[oi][:, 0:nq], Ops, [('ps', ob)], [('Osb', oi)])

            def fin1(anchor):
                RECIP(Osb[oi][64:128, 0:nq], Osb[oi][64:128, 0:nq], [('Osb', oi)], [('Osb', oi)])

            def fin2(anchor):
                shp = psum[0:64, 7 * 512: 7 * 512 + nq]
                P.extra = anchor
                P.cur_slack = 1.0
                MM(shp, shiftm, Osb[oi][:, 0:nq], True, True, [('Osb', oi), 'shiftm'], [('ps', 7)])
                P.extra = None
                TT('vector', attnT[0:64, h, qc0:qc0 + nq], Osb[oi][0:64, 0:nq], shp, ALU.mult,
                   [('Osb', oi), ('ps', 7)], [('attnT', h)])
                P.cur_slack = 0.0
            pending_fin.append((fin1, fin2))

        p2_loads(0)
        CDMA(Thi, THS, ['Thi'])
        CDMA(Tlo, TLS, ['Tlo'])
        p2_gate_all(0)
        P.extra = P.last
        CDMA(Kaug[1][64:96], onehot_d, [('Kaug1h', 1)])
        P.extra = None
        p2_loads(1)
        for h in range(8):
            for ci in range(9):
                if ci == 1 and h >= 1 and h + 1 < 8:
                    p2_loads(h + 1)
                hooks = {}
                if pending_fin:
                    f1, f2 = pending_fin.pop(0)
                    hooks.setdefault(1, []).append(f1)
                    hooks.setdefault(5, []).append(f2)
                p2_chunk(h, ci, hooks)
            if h + 1 < 8:
                p2_gate_all(h + 1)
        while pending_fin:
            f1, f2 = pending_fin.pop(0)
            f1(None)
            f2(None)
        P.emit()

        T3 = Alloc(85, 172)
        Wba = T3([8, 1024], BF16, parts=64)
        Wbp = T3([4, 1024], BF16)
        poolw = T3([4, 128], BF16)
        Wout = T3([8, 1024], BF16)
        Am = T3([16, 128], BF16)
        pscale = T3([4], F32)
        sgc = [T3([2, 512], BF16) for _ in range(2)]
        pooledT = T3([4, 128], BF16)
        pool2T = [T3([4, 512], BF16)] * 2
        t1 = T3([512], BF16)
        t2 = T3([512], BF16)
        mergedT = [T3([8, 512], BF16) for _ in range(2)]
        xo = [T3([1024], F32), modrep[:, 4096:5120]]
        x1t = [T3([1024], F32), modrep[:, 0:1024]]
        tmp3 = [T3([1024], F32), modrep[:, 1024:2048]]
        h2b = [T3([1024], BF16), a16[:, 12288:12288 + 1024]]
        small3 = T3([16], F32)

        CDMA(Am, amat_d, ['Am'])
        CDMA(pscale, pscale_d, ['pscale'])
        CDMA(poolw, poolw_d, ['poolw'], cast=True)
        CDMA(Wbp, wbp_d.rearrange("(g p) c -> p g c", p=128), ['Wbp'], cast=True)
        CDMA(Wba, wba_d.rearrange("(h d) c -> d h c", d=64), ['Wba'], cast=True)
        CDMA(Wout, wout_d.rearrange("(k p) c -> p k c", p=128), ['Wout'], cast=True)

        chunks = [(0, 1)] + [(1 + 4 * c, 4) for c in range(4)]

        def st_a(cix, tl):
            tf, ntl = chunks[cix]
            ti = tf + tl
            pb_ = cix % 2
            cb, pb = (0, 4) if ti != 1 else (8, 12)
            for g in range(4):
                o = bank(0)[:, g * 128:(g + 1) * 128]
                MM(o, p_own[:, ti, g * 128:(g + 1) * 128], Am[:, cb + g, :], True, ti == 0, ['p_own', 'Am'], [('ps', 0)])
                if ti >= 1:
                    MM(o, p_own[:, ti - 1, g * 128:(g + 1) * 128], Am[:, pb + g, :], False, True, ['p_own', 'Am'], [('ps', 0)])
            CP('scalar', pooledT, bank(0).rearrange("p (g t) -> p g t", t=128), [('ps', 0)], ['pooledT'])
            for g in range(4):
                MM(bank(0)[:, g * 128:(g + 1) * 128], poolw[:, g, :], pooledT[:, g, :], True, True,
                   ['poolw', 'pooledT'], [('ps', 0)])
            for g in range(4):
                ACT(pool2T[pb_][:, g, tl * 128:(tl + 1) * 128], bank(0)[:, g * 128:(g + 1) * 128], AF.Copy,
                    [('ps', 0), 'pscale'], [('pool2T', tl)], scale=pscale[:, g:g + 1])

        def st_b(cix, mt):
            tf, ntl = chunks[cix]
            N, tok0, pb_ = ntl * 128, tf * 128, cix % 2
            si = mt % 2
            DMA('sync', sgc[si][:, 0, 0:N], SGS[mt, :, tok0:tok0 + N], (), [('sgc', si)], 'sgc%d' % si)
            DMA('sync', sgc[si][:, 1, 0:N], SGS[8 + mt, :, tok0:tok0 + N], (), [('sgc', si)], 'sgc%d' % si)
            ba, bp = 2 + (mt % 2), 4 + (mt % 2)
            for hh in range(8):
                MM(bank(ba, N), Wba[0:64, hh, mt * 128:(mt + 1) * 128], attnT[0:64, hh, tok0:tok0 + N],
                   hh == 0, hh == 7, ['Wba', 'attnT'], [('ps', ba)])
            for g in range(4):
                MM(bank(bp, N), Wbp[:, g, mt * 128:(mt + 1) * 128], pool2T[pb_][:, g, 0:N], g == 0, g == 3,
                   ['Wbp', 'pool2T'], [('ps', bp)])
            TT('vector', t1[:, 0:N], bank(ba, N), sgc[si][:, 0, 0:N], ALU.mult, [('ps', ba), ('sgc', si)], ['t1'])
            TT('vector', t2[:, 0:N], bank(bp, N), sgc[si][:, 1, 0:N], ALU.mult, [('ps', bp), ('sgc', si)], ['t2'])
            TT('vector', mergedT[pb_][:, mt, 0:N], t1[:, 0:N], t2[:, 0:N], ALU.add, ['t1', 't2'], [('mergedT%d' % pb_, mt)])

        def st_c(cix, tl):
            tf, ntl = chunks[cix]
            pb_ = cix % 2
            ti = tf + tl
            bi = ti % 2
            DMA('sync', xo[bi], xw_v[47 + ti], (), [('xo', bi)], 'xo%d' % bi)
            for half in range(2):
                wb = 6 + half
                for k in range(8):
                    MM(bank(wb), mergedT[pb_][:, k, tl * 128:(tl + 1) * 128], Wout[:, k, half * 512:(half + 1) * 512],
                       k == 0, k == 7, ['mergedT%d' % pb_, 'Wout'], [('ps', wb)])
                TT('vector', tmp3[bi][:, half * 512:(half + 1) * 512], bank(wb), gate1rep[:, half * 512:(half + 1) * 512],
                   ALU.mult, [('ps', wb), ('modrep', 2)], [('tmp3_%d' % bi, half)])
            TT('vector', x1t[bi], tmp3[bi], xo[bi], ALU.add, ['tmp3_%d' % bi, ('xo', bi)], [('x1t', bi)])
            DMA('sync', X1S[ti], x1t[bi], [('x1t', bi)], [('X1S', ti)], 'x1s%d' % bi)
            sm = small3[:, (ti % 4) * 4:(ti % 4) * 4 + 4]
            smk = ('small3', ti % 4)
            ACT(h2b[bi], x1t[bi], AF.Square, [('x1t', bi)], [('h2b', bi), smk], accum_out=sm[:, 0:1])
            ACT(sm[:, 1:2], sm[:, 0:1], AF.Sqrt, [smk], [smk], scale=1.0 / 1024.0, bias=epsb)
            RECIP(sm[:, 2:3], sm[:, 1:2], [smk], [smk])
            STT(tmp3[bi], x1t[bi], sm[:, 2:3], G2rep, ALU.mult, ALU.mult, [('x1t', bi), smk, 'G2rep'], ['tmp3_%d' % bi])
            TT('vector', h2b[bi], tmp3[bi], shift2rep, ALU.add, ['tmp3_%d' % bi, ('modrep', 3)], [('h2b', bi)])
            for k in range(8):
                TR(bank16(0)[:, k * 128:(k + 1) * 128], h2b[bi][:, k * 128:(k + 1) * 128], ident_bf,
                   [('h2b', bi), 'ident_bf'], [('ps', 0)])
            CP('scalar', h2T[:, :, ti * 128:(ti + 1) * 128], bank16(0).rearrange("p (k t) -> p k t", t=128),
               [('ps', 0)], [('h2T', ti)])

        for tl in range(chunks[0][1]):
            st_a(0, tl)
        for mt in range(8):
            st_b(0, mt)
        for cix in range(len(chunks)):
            ntl = chunks[cix][1]
            nxt = cix + 1 if cix + 1 < len(chunks) else None
            if nxt is not None:
                for tl in range(chunks[nxt][1]):
                    st_a(nxt, tl)
            mts = list(range(8)) if nxt is not None else []
            per = (len(mts) + ntl - 1) // ntl if mts else 0
            for tl in range(ntl):
                st_c(cix, tl)
                for mt in mts[tl * per:(tl + 1) * per]:
                    st_b(nxt, mt)
        P.emit()

        mT = Alloc(34, 122)([22, 2048], BF16)
        T4 = Alloc(122, 172)
        Wup = [T4([8, 256], BF16) for _ in range(2)]
        abuf = [T4([2050], F32) for _ in range(2)]
        ubuf = [T4([2048], F32) for _ in range(2)]
        sgu = T4([2048], BF16)
        convw = T4([3, 44], F32)
        convb = T4([44], F32)
        CDMA(convw, convw_d, ['convw'])
        CDMA(convb, convb_d, ['convb'])
        wup_v = wup_d.rearrange("(k p) c -> p k c", p=128)
        arr = [0]
        for fp in range(22):
            wi = fp % 2
            DMA('gpsimd', Wup[wi][:, :, 0:128], wup_v[:, :, fp * 128:(fp + 1) * 128], (), [('Wup', wi)], 'wu%d' % wi)
            DMA('gpsimd', Wup[wi][:, :, 128:256], wup_v[:, :, 2816 + fp * 128:2816 + (fp + 1) * 128], (), [('Wup', wi)], 'wu%d' % wi)
            for which in range(2):
                f = fp + 22 * which
                ab = abuf[which]
                ak = ('abuf', which)
                hps = psum[:, 7 * 512 + which * 2: 7 * 512 + which * 2 + 2]
                for k in range(8):
                    MM(hps, Wup[wi][:, k, which * 128:(which + 1) * 128], h2T[:, k, 126:128], k == 0, k == 7,
                       [('Wup', wi), 'h2T'], [('ps', 7)])
                ACT(ab[:, 0:2], hps, AF.Copy, [('ps', 7), 'hflag'], [ak], scale=hflag)
                for c in range(4):
                    bi = arr[0] % 6
                    arr[0] += 1
                    for k in range(8):
                        MM(bank(bi), Wup[wi][:, k, which * 128:(which + 1) * 128], h2T[:, k, 128 + c * 512: 128 + (c + 1) * 512],
                           k == 0, k == 7, [('Wup', wi), 'h2T'], [('ps', bi)])
                    CP('scalar', ab[:, 2 + c * 512: 2 + (c + 1) * 512], bank(bi), [('ps', bi)], [ak])
                ub = ubuf[which]
                uk = ('ubuf', which)
                ACT(ub, ab[:, 2:2050], AF.Identity, [ak, 'convw', 'convb'], [uk], scale=convw[:, 2, f:f + 1], bias=convb[:, f:f + 1])
                STT(ub, ab[:, 1:2049], convw[:, 1, f:f + 1], ub, ALU.mult, ALU.add, [ak, 'convw', uk], [uk])
                STT(ub, ab[:, 0:2048], convw[:, 0, f:f + 1], ub, ALU.mult, ALU.add, [ak, 'convw', uk], [uk])
            ACT(sgu, ubuf[0], AF.Silu, [('ubuf', 0)], ['sgu'])
            TT('vector', mT[:, fp, :], sgu, ubuf[1], ALU.mult, ['sgu', ('ubuf', 1)], [('mT', fp)])
        P.emit()

        T5 = Alloc(122, 206)
        Wd = T5([22, 1024], BF16)
        x1l = [T5([1024], F32) for _ in range(2)]
        ot = [T5([1024], F32) for _ in range(2)]
        tmp5 = T5([1024], F32)
        wd_v = wdown_d.rearrange("(k p) c -> p k c", p=128)
        WDQ = [0, 4, 10, 16, 22]
        for q in range(4):
            CDMA(Wd[:, WDQ[q]:WDQ[q + 1], :], wd_v[:, WDQ[q]:WDQ[q + 1], :], [('Wd', q)], cast=True)
        for ti in range(1, NOWN):
            s = ti % 2
            DMA('sync', x1l[s], X1S[ti], (), [('x1l', s)], 'x1l%d' % s)
            for half in range(2):
                yb = (ti % 2) * 2 + half
                for i in range(22):
                    MM(bank(yb), mT[:, i, (ti - 1) * 128: ti * 128], Wd[:, i, half * 512:(half + 1) * 512],
                       i == 0, i == 21, ['mT', ('Wd', 0 if i < 4 else (1 if i < 10 else (2 if i < 16 else 3)))], [('ps', yb)])
                TT('vector', tmp5[:, half * 512:(half + 1) * 512], bank(yb), gate2rep[:, half * 512:(half + 1) * 512],
                   ALU.mult, [('ps', yb), ('modrep', 5)], [('tmp5', half)])
            TT('vector', ot[s], tmp5, x1l[s], ALU.add, ['tmp5', ('x1l', s)], [('ot', s)])
            DMA('sync', out_d[(ti - 1) * 128: ti * 128, :], ot[s], [('ot', s)], (), 'out%d' % s)
        P.emit(final=True)
    return nc


def _t5_bucket_np(d):
    n = np.maximum(d, 0)
    nf = np.maximum(n, 1).astype(np.float32)
    large = 16 + (np.log(nf / np.float32(16)) / np.float32(np.log(128 / 16)) * np.float32(16)).astype(np.int32)
    large = np.minimum(large, 31)
    return np.where(n < 16, n, large)


_NC_CACHE = {}


def kernel(x, c, ada_w, ada_b, norm1_g, w_in, q_norm_g, k_norm_g, rel_bias, pool_w,
           pool_scale, w_branch_attn, w_branch_pool, w_out, norm2_g, w_up, conv_w, conv_b, w_down):
    f = lambda a: np.ascontiguousarray(np.asarray(a, dtype=np.float32))
    x = f(x); c = f(c)
    ada_w = f(ada_w)[0]; ada_b = f(ada_b)[0]; norm1_g = f(norm1_g)[0]; w_in = f(w_in)[0]
    q_norm_g = f(q_norm_g)[0]; k_norm_g = f(k_norm_g)[0]; rel_bias = f(rel_bias)
    pool_w = f(pool_w)[0]; pool_scale = f(pool_scale)[0]
    w_branch_attn = f(w_branch_attn)[0]; w_branch_pool = f(w_branch_pool)[0]; w_out = f(w_out)[0]
    norm2_g = f(norm2_g)[0]; w_up = f(w_up)[0]; conv_w = f(conv_w)[0]; conv_b = f(conv_b)[0]; w_down = f(w_down)[0]

    if 'nc' not in _NC_CACHE:
        _NC_CACHE['nc'] = build_nc()
    nc = _NC_CACHE['nc']

    kl = np.arange(128)[:, None]
    jj = np.arange(640)[None, :]
    dd = jj - 128 - kl
    bidx = _t5_bucket_np(dd)
    traw = np.zeros((128, 8, 640), np.float32)
    for h in range(8):
        traw[:, h, :] = np.where(dd >= 0, rel_bias[bidx, h], 0.0)
    cmask = np.where(dd >= 0, 0.0, NEG).astype(np.float32)
    rb31 = np.ascontiguousarray(np.broadcast_to(rel_bias[31][None, :], (128, 8))).astype(np.float32)
    onehot = np.zeros((32, 8192), np.float32)
    for n in range(32):
        onehot[n, n * 256:(n + 1) * 256] = 1.0
    onehot = onehot.astype(ml_dtypes.bfloat16)
    shiftm = np.zeros((128, 64), np.float32)
    shiftm[64 + np.arange(64), np.arange(64)] = 1.0
    wins = [2, 4, 8, 16]
    tp = np.arange(128)[:, None]
    tt = np.arange(128)[None, :]
    A_std = np.zeros((128, 16, 128), np.float32)
    for g, w in enumerate(wins):
        cur = np.where((tt - tp >= 0) & (tt - tp < w), 1.0 / w, 0.0) - (tp == tt)
        prev = np.where((tt + 128 - tp) < w, 1.0 / w, 0.0)
        A_std[:, g, :] = cur
        A_std[:, 4 + g, :] = prev
        A_std[:, 8 + g, :] = cur
        A_std[:, 12 + g, :] = prev
    A_first = A_std.copy()
    for g, w in enumerate(wins):
        cnt = np.minimum(tt + 1, w).astype(np.float32)
        cur = np.where((tt - tp >= 0) & (tt - tp < w), 1.0 / cnt, 0.0) - (tp == tt)
        A_first[:, 8 + g, :] = cur
        A_first[:, 12 + g, :] = 0.0
    rep = lambda v: np.ascontiguousarray(np.broadcast_to(v[None, :], (128, v.shape[0]))).astype(np.float32)
    common = {
        "ada_w": ada_w, "adab_rep": rep(ada_b), "g1rep": rep(norm1_g), "g2rep": rep(norm2_g),
        "w_in": w_in, "gkcol": np.ascontiguousarray(np.concatenate([k_norm_g, k_norm_g])[:, None]),
        "gqcol": np.ascontiguousarray(np.concatenate([q_norm_g, q_norm_g])[:, None]),
        "traw": traw, "cmask": cmask, "rb31": rb31, "onehot": onehot,
        "poolw": np.ascontiguousarray(pool_w.transpose(1, 0, 2)),
        "pscale": np.ascontiguousarray(pool_scale.reshape(4, 128).T),
        "wba": w_branch_attn, "wbp": w_branch_pool, "wout": w_out, "wup": w_up,
        "convw": np.ascontiguousarray(conv_w.reshape(3, 44, 128).transpose(2, 0, 1)),
        "convb": np.ascontiguousarray(conv_b.reshape(44, 128).T),
        "wdown": w_down, "shiftm": shiftm,
    }
    in_maps = []
    for core in range(8):
        b, j = core // 4, core % 4
        end = 2048 * (j + 1)
        start = end - 8192
        xwin = np.zeros((8192, 1024), np.float32)
        if start >= 0:
            xwin[:] = x[b, start:end]
        else:
            xwin[-start:] = x[b, 0:end]
        first_valid = 24 - 8 * j
        vb = np.zeros((128, 17, 32), np.float32)
        no = np.ones((128, 17, 32), np.float32)
        for ti in range(17):
            bb = (47 + ti) // 2
            row = np.full(32, -1e30, np.float32)
            row[first_valid:bb] = 0.0
            vb[:, ti, :] = row[None, :]
            no[:, ti, bb] = 0.0
        m = dict(common)
        m["xw"] = xwin
        m["cT"] = np.ascontiguousarray(c[b].reshape(8, 128).T)
        m["validb17"] = vb
        m["notown17"] = no.astype(ml_dtypes.bfloat16)
        m["amats"] = (A_first if j == 0 else A_std).astype(ml_dtypes.bfloat16)
        m["hflag"] = np.full((128, 1), 0.0 if j == 0 else 1.0, np.float32)
        in_maps.append(m)
    res = run_bass_kernel_spmd(nc, in_maps, core_ids=list(range(8)))
    out = np.zeros((2, 8192, 1024), np.float32)
    for core in range(8):
        b, j = core // 4, core % 4
        out[b, 2048 * j:2048 * (j + 1)] = res.results[core]["out"]
    return out
```
